# Optimizing a Trainium2 kernel written in Bass

```python
import jax, jax.numpy as jnp
from jax import lax
import numpy as np

D_MODEL = 1024
BATCH = 2
SEQ = 8192
DEPTH = 1

D_FF = 2816
D_A = D_MODEL
N_GROUPS_A = 8
CONV_K = 3
D_B = D_MODEL
N_HEADS_B = 8
DH_B = D_B // N_HEADS_B
CHUNK = 128
N_SUBLAYERS = 3
EPS = 1e-6
MACARON_W = 0.5
MIX_IN_WIDTHS = (D_MODEL, D_MODEL, D_A, D_A, D_A, D_B, D_B)
MIX_IN = sum(MIX_IN_WIDTHS)

kernel_name = "hybrid_conv_gmlp_macaron_adaln"


def rmsnorm(x, g):
    xf = x.astype(jnp.float32)
    y = xf * lax.rsqrt(jnp.mean(xf * xf, axis=-1, keepdims=True) + EPS)
    return (y * g.astype(jnp.float32)).astype(x.dtype)


def layernorm(x, g, b):
    xf = x.astype(jnp.float32)
    mu = jnp.mean(xf, axis=-1, keepdims=True)
    var = jnp.mean(jnp.square(xf - mu), axis=-1, keepdims=True)
    y = (xf - mu) * lax.rsqrt(var + EPS)
    return (y * g.astype(jnp.float32) + b.astype(jnp.float32)).astype(x.dtype)


def modulate(h, shift, scale):
    return h * (1.0 + scale[:, None, :]) + shift[:, None, :]


def swiglu(h, w_in, w_out):
    a, b = jnp.split(h @ w_in, 2, axis=-1)
    return (jax.nn.silu(a) * b) @ w_out


def causal_dwconv(z, w):
    k, ch = w.shape
    return lax.conv_general_dilated(
        z, w[:, None, :].astype(z.dtype), window_strides=(1,),
        padding=[(k - 1, 0)], dimension_numbers=("NWC", "WIO", "NWC"),
        feature_group_count=ch)


def chunked_spatial_gate(u, v, w_spatial, b_spatial):
    bsz, s, _ = v.shape
    mask = jnp.tril(jnp.ones((CHUNK, CHUNK), dtype=w_spatial.dtype))
    w_sp = w_spatial * mask[None]
    vh = v.reshape(bsz, s // CHUNK, CHUNK, N_HEADS_B, DH_B)
    sv = jnp.einsum("hts,bnshd->bnthd", w_sp, vh) + b_spatial.T[None, None, :, :, None]
    return u * sv.reshape(bsz, s, D_B)


def setup_inputs(seed: int = 0) -> dict:
    key = jax.random.key(seed)
    ks = jax.random.split(key, 24)
    f32 = jnp.float32
    n = lambda k, shape, s: jax.random.normal(k, shape, f32) * s
    gain = lambda k, d: 1.0 + 0.02 * jax.random.normal(k, (d,), f32)
    return {
        "x": jax.random.normal(ks[0], (BATCH, SEQ, D_MODEL), f32),
        "c": jax.random.normal(ks[1], (BATCH, D_MODEL), f32),
        "w_ada": n(ks[2], (D_MODEL, N_SUBLAYERS * 3 * D_MODEL), 0.5 * D_MODEL ** -0.5),
        "b_ada": n(ks[3], (N_SUBLAYERS * 3 * D_MODEL,), 0.02),
        "g_ffn1": gain(ks[4], D_MODEL),
        "w_ffn1_in": n(ks[5], (D_MODEL, 2 * D_FF), D_MODEL ** -0.5),
        "w_ffn1_out": n(ks[6], (D_FF, D_MODEL), D_FF ** -0.5),
        "g_mix": gain(ks[7], D_MODEL),
        "w_mix_in": n(ks[8], (D_MODEL, MIX_IN), D_MODEL ** -0.5),
        "conv_w": n(ks[9], (CONV_K, D_A), CONV_K ** -0.5),
        "ln_v_g": gain(ks[10], D_B),
        "ln_v_b": n(ks[11], (D_B,), 0.02),
        "w_spatial": n(ks[12], (N_HEADS_B, CHUNK, CHUNK), CHUNK ** -0.5),
        "b_spatial": 1.0 + n(ks[13], (N_HEADS_B, CHUNK), 0.02),
        "w_a_out": n(ks[14], (D_A, D_MODEL), D_A ** -0.5),
        "w_b_out": n(ks[15], (D_B, D_MODEL), D_B ** -0.5),
        "w_mix_out": n(ks[16], (D_MODEL, D_MODEL), D_MODEL ** -0.5),
        "g_ffn2": gain(ks[17], D_MODEL),
        "w_ffn2_in": n(ks[18], (D_MODEL, 2 * D_FF), D_MODEL ** -0.5),
        "w_ffn2_out": n(ks[19], (D_FF, D_MODEL), D_FF ** -0.5),
        "g_final": gain(ks[20], D_MODEL),
    }


def reference(x, c, w_ada, b_ada, g_ffn1, w_ffn1_in, w_ffn1_out, g_mix, w_mix_in,
              conv_w, ln_v_g, ln_v_b, w_spatial, b_spatial, w_a_out, w_b_out,
              w_mix_out, g_ffn2, w_ffn2_in, w_ffn2_out, g_final):
    split_pts = list(np.cumsum(MIX_IN_WIDTHS)[:-1])
    for _ in range(DEPTH):
        ada = jax.nn.silu(c) @ w_ada + b_ada
        (sh1, sc1, gt1, sh2, sc2, gt2, sh3, sc3, gt3) = jnp.split(ada, 3 * N_SUBLAYERS, axis=-1)

        h = modulate(rmsnorm(x, g_ffn1), sh1, sc1)
        x = x + MACARON_W * gt1[:, None, :] * swiglu(h, w_ffn1_in, w_ffn1_out)

        h = modulate(rmsnorm(x, g_mix), sh2, sc2)
        p = h @ w_mix_in
        gate_a, gate_b, b_a, c_a, x_a, u_b, v_b = jnp.split(p, split_pts, axis=-1)
        y_a = (b_a * causal_dwconv(c_a * x_a, conv_w)) @ w_a_out
        u_b = jax.nn.gelu(u_b)
        v_b = layernorm(jax.nn.gelu(v_b), ln_v_g, ln_v_b)
        y_b = chunked_spatial_gate(u_b, v_b, w_spatial, b_spatial) @ w_b_out
        m = (jax.nn.sigmoid(gate_a) * y_a + jax.nn.sigmoid(gate_b) * y_b) @ w_mix_out
        x = x + gt2[:, None, :] * m

        h = modulate(rmsnorm(x, g_ffn2), sh3, sc3)
        x = x + MACARON_W * gt3[:, None, :] * swiglu(h, w_ffn2_in, w_ffn2_out)
    return rmsnorm(x, g_final)
```

```python
import numpy as np
from contextlib import ExitStack
import concourse.bass as bass
import concourse.mybir as mybir
from concourse.bass_utils import run_bass_kernel_spmd

F32 = mybir.dt.float32
BF16 = mybir.dt.bfloat16
AF = mybir.ActivationFunctionType
ALU = mybir.AluOpType

D = 1024
DC = 8
TOK = 2048
TH = 2050
DFF = 2816
FC = 22
FH = 11
EPS = 1e-6
NCORES = 8
SEQ = 8192
TILES = [(0, 512), (512, 512), (1024, 512), (1536, 512), (2048, 2)]
HALO = 4

V_G1, V_GM, V_G2, V_GF, V_LNG, V_LNB, V_CW, V_BADA, V_C = 0, 8, 16, 24, 32, 40, 48, 72, 144
NV = 152


class UnitMeta:
    def __init__(self, off, kc, w):
        self.off, self.kc, self.w = off, kc, w
        self.L = kc * w


def build_weight_stream(W):
    units = []
    bufs = []
    pos = [0]

    def add_unit(mats):
        kc = len(mats[0][1])
        blocks = []
        for (M, rcs, c0) in mats:
            rows = np.concatenate([np.arange(r * 128, r * 128 + 128) for r in rcs])
            blk = M[rows, c0:c0 + 128].reshape(kc, 128, 128).transpose(1, 0, 2)
            blocks.append(blk)
        u = np.concatenate(blocks, axis=2)
        w = u.shape[2]
        units.append(UnitMeta(pos[0], kc, w))
        bufs.append(np.ascontiguousarray(u, dtype=np.float32).reshape(-1))
        pos[0] += 128 * kc * w
        return len(units) - 1

    K8 = list(range(8))
    plan = {}
    plan["ada"] = []
    for part in range(3):
        us = []
        for q in range(6):
            c0 = part * 3072 + q * 512
            us.append(add_unit([(W["w_ada"], K8, c0 + 128 * i) for i in range(4)]))
        plan["ada"].append(us)
    for name, win, wout in (("ffn1", "w_ffn1_in", "w_ffn1_out"), ("ffn2", "w_ffn2_in", "w_ffn2_out")):
        halves = []
        for hh in range(2):
            seq = []
            for j in range(hh * FH, hh * FH + FH):
                seq.append(("a", j, (W[win], K8, j * 128)))
                seq.append(("b", j, (W[win], K8, DFF + j * 128)))
            in_units = []
            for i in range(0, len(seq), 4):
                grp = seq[i:i + 4]
                uid = add_unit([g[2] for g in grp])
                in_units.append((uid, [(g[0], g[1]) for g in grp]))
            out_units = []
            rcs = list(range(hh * FH, hh * FH + FH))
            for m0 in range(0, 8, 2):
                uid = add_unit([(W[wout], rcs, m * 128) for m in (m0, m0 + 1)])
                out_units.append((uid, [m0, m0 + 1]))
            halves.append((in_units, out_units))
        plan[name] = halves
    wm = W["w_mix_in"]
    O_GA, O_GB, O_B, O_C, O_X, O_U, O_V = [i * 1024 for i in range(7)]
    plan["v"] = [add_unit([(wm, K8, O_V + uh * 512 + 128 * i) for i in range(4)]) for uh in range(2)]
    plan["u"] = [add_unit([(wm, K8, O_U + uu * 512 + 128 * i) for i in range(4)]) for uu in range(2)]
    plan["cxb"] = [add_unit([(wm, K8, O_C + j * 128), (wm, K8, O_X + j * 128), (wm, K8, O_B + j * 128)])
                   for j in range(8)]
    plan["p3"] = [add_unit([(wm, K8, O_GA + m * 128), (wm, K8, O_GB + m * 128),
                            (W["w_a_out"], K8, m * 128), (W["w_b_out"], K8, m * 128)]) for m in range(8)]
    plan["mo"] = [add_unit([(W["w_mix_out"], K8, (4 * q + i) * 128) for i in range(4)]) for q in range(2)]
    stream = np.concatenate(bufs)
    return stream, units, plan


class Ctr:
    def __init__(self, sem, step):
        self.sem, self.step, self.val = sem, step, 0


class Res:
    __slots__ = ("w", "r")

    def __init__(self):
        self.w = None
        self.r = []


class Q:
    def __init__(self, eng, ctr=None):
        self.eng, self.ctr, self.known = eng, ctr, {}

    def wait(self, ctr, val):
        if self.known.get(ctr, 0) >= val:
            return
        self.eng.wait_ge(ctr.sem, val)
        self.known[ctr] = val


def _deps(q, reads, writes):
    deps = {}
    for r in reads:
        if r.w is not None:
            c, v = r.w
            deps[c] = max(deps.get(c, 0), v)
    for w in writes:
        if w.w is not None:
            c, v = w.w
            deps[c] = max(deps.get(c, 0), v)
        for (c, v) in w.r:
            deps[c] = max(deps.get(c, 0), v)
    for c, v in deps.items():
        q.wait(c, v)


def emit(q, fn, reads=(), writes=(), ctr=None):
    _deps(q, reads, writes)
    ins = fn()
    c = ctr if ctr is not None else q.ctr
    c.val += c.step
    ins.then_inc(c.sem, c.step)
    tok = (c, c.val)
    for r in reads:
        r.r.append(tok)
    for w in writes:
        w.w = tok
        w.r = []
    return tok


class Ring:
    def __init__(self, aps):
        self.aps = aps
        self.res = [Res() for _ in aps]
        self.i = 0

    def next(self):
        k = self.i % len(self.aps)
        self.i += 1
        return self.aps[k], self.res[k]


def build_program(units, plan, total_w):
    nc = bass.Bass("TRN2", target_bir_lowering=False)
    xT_d = nc.dram_tensor("xT", [128, DC, TH], F32, kind="ExternalInput").ap()
    vecs_d = nc.dram_tensor("vecs", [128, NV], F32, kind="ExternalInput").ap()
    wsp_d = nc.dram_tensor("wsp", [128, 8, 128], F32, kind="ExternalInput").ap()
    bsp_d = nc.dram_tensor("bsp", [128, 1024], F32, kind="ExternalInput").ap()
    hm_d = nc.dram_tensor("hmask", [128, 1], F32, kind="ExternalInput").ap()
    wst_d = nc.dram_tensor("wst", [total_w], F32, kind="ExternalInput").ap()
    yT_d = nc.dram_tensor("yT", [128, DC, TOK], F32, kind="ExternalOutput").ap()

    with ExitStack() as es:
        E = es.enter_context
        x = E(nc.sbuf_tensor("x", [128, DC, TH], F32))
        h = E(nc.sbuf_tensor("h", [128, DC, TH], BF16))
        regB = E(nc.sbuf_tensor("regB", [128, 24576], BF16))
        regF = E(nc.sbuf_tensor("regF", [128, 2 * 1026], F32))
        wslots = [E(nc.sbuf_tensor(f"wslot{i}", [128, 4096], BF16)) for i in range(3)]
        tA = [E(nc.sbuf_tensor(f"tA{i}", [128, 512], F32)) for i in range(4)]
        tD = [E(nc.sbuf_tensor(f"tD{i}", [128, 512], F32)) for i in range(3)]
        vecs = E(nc.sbuf_tensor("vecs_sb", [128, NV], F32))
        adaT = E(nc.sbuf_tensor("adaT", [128, 72], F32))
        der = E(nc.sbuf_tensor("der", [128, 48], F32))
        small = E(nc.sbuf_tensor("small", [128, 64], F32))
        hm = E(nc.sbuf_tensor("hm", [128, 1], F32))
        sc_bf = E(nc.sbuf_tensor("sc_bf", [128, 8, 2], BF16))
        ones_bf = E(nc.sbuf_tensor("ones_bf", [128, 128], BF16))
        ones_f = E(nc.sbuf_tensor("ones_f", [128, 128], F32))
        wspT = E(nc.sbuf_tensor("wspT", [128, 8, 128], BF16))
        biast = E(nc.sbuf_tensor("biast", [128, 8, 128], F32))
        zcar = E(nc.sbuf_tensor("zcar", [128, 8, 2], F32))
        stats = E(nc.sbuf_tensor("stats", [128, 2, 2, 6], F32))
        banks = [E(nc.psum_tensor(f"ps{i}", [128, 512], F32)) for i in range(8)]

        s_pe = E(nc.semaphore("s_pe"))
        s_act = E(nc.semaphore("s_act"))
        s_dve = E(nc.semaphore("s_dve"))
        s_pool = E(nc.semaphore("s_pool"))
        s_ld = E(nc.semaphore("s_ld"))
        s_x = [E(nc.semaphore(f"s_x{i}")) for i in range(DC)]
        s_st = E(nc.semaphore("s_st"))
        s_w = [E(nc.semaphore(f"s_w{i}")) for i in range(3)]

        c_pe, c_act, c_dve, c_pool = Ctr(s_pe, 1), Ctr(s_act, 1), Ctr(s_dve, 1), Ctr(s_pool, 1)
        c_ld, c_st = Ctr(s_ld, 16), Ctr(s_st, 16)
        c_x = [Ctr(s, 16) for s in s_x]
        c_w = [Ctr(s, 16) for s in s_w]

        streams = {"pe": [], "act": [], "dve": [], "pool": [], "sp": []}

        class Rec:
            def __init__(self, name):
                self.name = name

            def __getattr__(self, meth):
                def call(*a, **kw):
                    item = {"m": meth, "a": a, "kw": kw, "inc": None}
                    streams[self.name].append(item)

                    class H:
                        def then_inc(_s, sem, n):
                            item["inc"] = (sem, n)
                            return _s
                    return H()
                return call

        PE, ACT, DVE, POOL, SP = (Q(Rec("pe"), c_pe), Q(Rec("act"), c_act), Q(Rec("dve"), c_dve),
                                  Q(Rec("pool"), c_pool), Q(Rec("sp"), None))

        r_x = [[Res() for _ in TILES] for _ in range(DC)]
        r_h = [[Res() for _ in TILES] for _ in range(DC)]
        r_sq = [[Res() for _ in TILES] for _ in range(DC)]
        r_g = [[Res() for _ in TILES] for _ in range(FH)]
        r_bank = [Res() for _ in range(8)]
        r_wslot = [Res() for _ in range(3)]
        r_const = Res()
        r_sc = Res()
        r_ada = [Res() for _ in range(3)]
        r_der = [Res() for _ in range(3)]
        r_wsp = Res()
        r_bspb = Res()
        r_wspT = Res()
        r_biast = Res()
        r_ones = Res()
        r_small = Res()
        r_stats = [Res(), Res()]
        r_ln = [Res(), Res()]
        r_vn = [Res() for _ in range(8)]
        r_gb = [[Res(), Res()] for _ in range(8)]
        r_za = [[Res(), Res()] for _ in range(8)]
        r_mg = [[Res(), Res()] for _ in range(8)]
        r_zb = [Res(), Res()]
        r_zcar = [Res() for _ in range(8)]
        ringA = Ring([t[:] for t in tA])
        ringD = Ring([t[:] for t in tD])

        def g_ap(jj, c0, n):
            return regB[:, jj * TH + c0: jj * TH + c0 + n]

        def sq_ap(j, c0, n):
            return regB[:, j * TH + c0: j * TH + c0 + n]

        def vn_ap(i, c0, n):
            return regB[:, i * 1024 + c0: i * 1024 + c0 + n]

        def mg_ap(m, c0, n):
            return regB[:, m * 1024 + c0: m * 1024 + c0 + n]

        def gb_ap(hd, c0, n):
            return regB[:, 8192 + hd * 1024 + c0: 8192 + hd * 1024 + c0 + n]

        def za_ap(j, c0, n):
            return regB[:, 16384 + j * 1024 + c0: 16384 + j * 1024 + c0 + n]

        def zb_ap(s, c0, n):
            return regF[:, s * 1026 + c0: s * 1026 + c0 + n]

        wsp_f = regF[:, 0:1024]
        bsp_b = regF[:, 1024:2048]

        r_regB_phase = Res()
        r_regF_phase = Res()

        job_ctr = [0]
        wstate = {"next_slot": 0, "unit_slot": {}, "uses": 0}

        def load_unit(uid):
            s = wstate["next_slot"] % 3
            wstate["next_slot"] += 1
            um = units[uid]
            src = wst_d[um.off: um.off + 128 * um.L].rearrange("(p k w) -> p k w", p=128, k=um.kc)
            dst = wslots[s][:, 0:um.L].rearrange("p (k w) -> p k w", k=um.kc)
            wstate["last_tok"] = emit(POOL, lambda: POOL.eng.dma_start(out=dst, in_=src), reads=[],
                                      writes=[r_wslot[s]], ctr=c_w[s])
            return s

        def wchunk(s, uid, k, mi):
            um = units[uid]
            o = k * um.w + mi * 128
            return wslots[s][:, o:o + 128]

        def pe_job(mms, reads, bank=None):
            if bank is None:
                b = job_ctr[0] % 6
                job_ctr[0] += 1
            else:
                b = bank
            _deps(PE, reads, [r_bank[b]])
            last = None
            for (o, l, r, st, sp) in mms:
                last = PE.eng.matmul(o(b), l, r, start=st, stop=sp)
            c_pe.val += 1
            last.then_inc(c_pe.sem, 1)
            tok = (c_pe, c_pe.val)
            for r in reads:
                r.r.append(tok)
            r_bank[b].w = tok
            r_bank[b].r = []
            return b

        def proj_job(s, uid, mi, kc, rhs_fn, rhs_res, n, extra_reads=()):
            mms = []
            for k in range(kc):
                mms.append((lambda b, n=n: banks[b][:, 0:n], wchunk(s, uid, k, mi), rhs_fn(k), k == 0, k == kc - 1))
            return pe_job(mms, [r_wslot[s]] + list(rhs_res) + list(extra_reads))

        emit(SP, lambda: SP.eng.dma_start(out=wsp_f.rearrange("p (a b) -> p a b", a=8), in_=wsp_d[:, :, :]),
             writes=[r_wsp], ctr=c_ld)
        emit(SP, lambda: SP.eng.dma_start(out=bsp_b, in_=bsp_d[:, :]), writes=[r_bspb], ctr=c_ld)
        emit(SP, lambda: SP.eng.dma_start(out=vecs[:], in_=vecs_d[:, :]), writes=[r_const], ctr=c_ld)
        emit(SP, lambda: SP.eng.dma_start(out=hm[:], in_=hm_d[:, :]), writes=[r_const], ctr=c_ld)
        tot = (c_ld, c_ld.val)
        for r in [r_const, r_wsp, r_bspb]:
            r.w = tot
        def load_x(t, after=None):
            c0, n = TILES[t]
            if after is not None:
                SP.wait(*after)
            emit(SP, lambda: SP.eng.dma_start(out=x[:, :, c0:c0 + n], in_=xT_d[:, :, c0:c0 + n]),
                 writes=[r_x[j][t] for j in range(DC)], ctr=c_x[t])

        emit(DVE, lambda: DVE.eng.memset(ones_bf[:], 1.0), writes=[r_ones])
        emit(DVE, lambda: DVE.eng.memset(ones_f[:], 1.0), writes=[r_ones])
        emit(DVE, lambda: DVE.eng.memset(small[:, 0:1], EPS), writes=[r_small])
        for dup in range(2):
            emit(ACT, lambda dup=dup: ACT.eng.activation(out=sc_bf[:, :, dup], in_=vecs[:, V_C:V_C + 8], func=AF.Silu),
                 reads=[r_const], writes=[r_sc])
        G_COL = [V_G1, V_GM, V_G2]

        ada_tok = {}
        r_adaS = [Res() for _ in range(3)]
        r_adaG = [Res() for _ in range(3)]

        def ada_units(part, qs):
            for q in qs:
                uid = plan["ada"][part][q]
                s = load_unit(uid)
                ada_tok[(part, q)] = wstate["last_tok"]
                mms = []
                for i in range(4):
                    for k in range(8):
                        mms.append((lambda b, i=i: banks[b][:, 2 * i:2 * i + 2], wchunk(s, uid, k, i),
                                    sc_bf[:, k, :], k == 0, k == 7))
                b = pe_job(mms, [r_wslot[s], r_sc])
                c0 = part * 24 + q * 4
                rr = r_adaS[part] if q < 4 else r_adaG[part]
                emit(DVE, lambda b=b, c0=c0: DVE.eng.tensor_tensor(
                    out=adaT[:, c0:c0 + 4], in0=banks[b][:, 0:8].rearrange("p (a b) -> p a b", b=2)[:, :, 0],
                    in1=vecs[:, V_BADA + c0:V_BADA + c0 + 4], op=ALU.add),
                    reads=[r_bank[b], r_const], writes=[rr])
            a0 = part * 24
            if 3 in qs:
                emit(DVE, lambda: DVE.eng.scalar_tensor_tensor(
                    out=der[:, part * 16:part * 16 + 8], in0=adaT[:, a0 + 8:a0 + 16], scalar=1.0,
                    in1=vecs[:, G_COL[part]:G_COL[part] + 8], op0=ALU.add, op1=ALU.mult),
                    reads=[r_adaS[part], r_const], writes=[r_adaS[part]])
            if 5 in qs:
                emit(DVE, lambda: DVE.eng.tensor_scalar(
                    out=der[:, part * 16 + 8:part * 16 + 16], in0=adaT[:, a0 + 16:a0 + 24],
                    scalar1=(1.0 if part == 1 else 0.5), scalar2=None, op0=ALU.mult),
                    reads=[r_adaG[part]], writes=[r_adaG[part]])

        def v_scale(part, j):
            return der[:, part * 16 + j: part * 16 + j + 1]

        def v_gtw(part, j):
            return der[:, part * 16 + 8 + j: part * 16 + 8 + j + 1]

        def v_shift(part, j):
            return adaT[:, part * 24 + j: part * 24 + j + 1]

        def sq_emit(j, t):
            c0, n = TILES[t]
            emit(ACT, lambda: ACT.eng.activation(out=h[:, j, c0:c0 + n], in_=x[:, j, c0:c0 + n], func=AF.Square),
                 reads=[r_x[j][t]], writes=[r_h[j][t]])

        class NormPipe:
            def __init__(self, part, final=False):
                self.part, self.final, self.prev, self.cnt = part, final, None, 0

            def tile_done(self, t):
                if self.prev is not None:
                    self._emit(self.prev)
                self.prev = t

            def flush(self):
                if self.prev is not None:
                    self._emit(self.prev)
                    self.prev = None

            def _emit(self, t):
                nb = 6 + (self.cnt % 2)
                self.cnt += 1
                self._ab(t, nb)
                self._c(t, banks[nb], r_bank[nb])

            def _ss(self, t, bank):
                c0, n = TILES[t]
                mms = []
                for j in range(DC):
                    mms.append((lambda b: banks[b][:, 0:n], ones_bf[:, :], h[:, j, c0:c0 + n], j == 0, j == DC - 1))
                return pe_job(mms, [r_ones] + [r_h[j][t] for j in range(DC)], bank=bank)

            def _ab(self, t, nb):
                c0, n = TILES[t]
                self._ss(t, nb)
                pb = banks[nb]
                emit(ACT, lambda: ACT.eng.activation(out=pb[:, 0:n], in_=pb[:, 0:n], func=AF.Sqrt,
                                                      bias=small[:, 0:1], scale=1.0 / D),
                     reads=[r_bank[nb], r_small], writes=[r_bank[nb]])
                emit(DVE, lambda: DVE.eng.reciprocal(out=pb[:, 0:n], in_=pb[:, 0:n]),
                     reads=[r_bank[nb]], writes=[r_bank[nb]])

            def _ab_sb(self, t, dst, dres, extra_w):
                c0, n = TILES[t]
                b = self._ss(t, None)
                pb = banks[b]
                emit(ACT, lambda: ACT.eng.activation(out=pb[:, 0:n], in_=pb[:, 0:n], func=AF.Sqrt,
                                                      bias=small[:, 0:1], scale=1.0 / D),
                     reads=[r_bank[b], r_small], writes=[r_bank[b]])
                emit(DVE, lambda: DVE.eng.reciprocal(out=dst[:, 0:n], in_=pb[:, 0:n]),
                     reads=[r_bank[b]], writes=[dres] + list(extra_w))

            def _c(self, t, rr, rres, extra_r=()):
                part, final = self.part, self.final
                c0, n = TILES[t]
                for j in range(DC):
                    if final:
                        emit(DVE, lambda j=j: DVE.eng.scalar_tensor_tensor(
                            out=x[:, j, c0:c0 + n], in0=x[:, j, c0:c0 + n], scalar=vecs[:, V_GF + j:V_GF + j + 1],
                            in1=rr[:, 0:n], op0=ALU.mult, op1=ALU.mult),
                            reads=[r_x[j][t], rres, r_const], writes=[r_x[j][t]])
                        if j % 2 == 1:
                            emit(SP, lambda j=j: SP.eng.dma_start(out=yT_d[:, j - 1:j + 1, c0:c0 + n],
                                                                   in_=x[:, j - 1:j + 1, c0:c0 + n]),
                                 reads=[r_x[j - 1][t], r_x[j][t]], ctr=c_st)
                    else:
                        ht, htres = ringD.next()
                        emit(DVE, lambda j=j, ht=ht: DVE.eng.scalar_tensor_tensor(
                            out=ht[:, 0:n], in0=x[:, j, c0:c0 + n], scalar=v_scale(part, j),
                            in1=rr[:, 0:n], op0=ALU.mult, op1=ALU.mult),
                            reads=[r_x[j][t], rres, r_adaS[part]] + list(extra_r), writes=[htres])
                        emit(ACT, lambda j=j, ht=ht: ACT.eng.activation(
                            out=h[:, j, c0:c0 + n], in_=ht[:, 0:n], func=AF.Identity,
                            bias=v_shift(part, j), scale=1.0),
                            reads=[htres, r_adaS[part]], writes=[r_h[j][t]])

        def ffn(name, part, tiles, nxt, pre_tile=None, post_unit=None):
            first = True
            for hh in range(2):
                in_units, out_units = plan[name][hh]
                for (uid, chunks) in in_units:
                    s = load_unit(uid)
                    for t in tiles:
                        c0, n = TILES[t]
                        st_ap = None
                        for mi, (kind, j) in enumerate(chunks):
                            b = proj_job(s, uid, mi, 8, lambda k: h[:, k, c0:c0 + n],
                                         [r_h[k][t] for k in range(DC)], n)
                            if kind == "a":
                                st_ap, st_res = ringA.next()
                                emit(ACT, lambda b=b, st_ap=st_ap: ACT.eng.activation(
                                    out=st_ap[:, 0:n], in_=banks[b][:, 0:n], func=AF.Silu),
                                    reads=[r_bank[b]], writes=[st_res])
                            else:
                                jj = j - hh * FH
                                emit(DVE, lambda b=b, st_ap=st_ap, jj=jj: DVE.eng.tensor_tensor(
                                    out=g_ap(jj, c0, n), in0=banks[b][:, 0:n], in1=st_ap[:, 0:n], op=ALU.mult),
                                    reads=[r_bank[b], st_res], writes=[r_g[jj][t], r_regB_phase])
                        if first and pre_tile is not None:
                            pre_tile(t)
                    first = False
                    if post_unit is not None:
                        post_unit()

                def out_jobs(s, uid, ms, t):
                    c0, n = TILES[t]
                    for mi, m in enumerate(ms):
                        b = proj_job(s, uid, mi, FH, lambda k: g_ap(k, c0, n),
                                     [r_g[k][t] for k in range(FH)], n)
                        emit(DVE, lambda b=b, m=m: DVE.eng.scalar_tensor_tensor(
                            out=x[:, m, c0:c0 + n], in0=banks[b][:, 0:n], scalar=v_gtw(part, m),
                            in1=x[:, m, c0:c0 + n], op0=ALU.mult, op1=ALU.add),
                            reads=[r_bank[b], r_x[m][t], r_adaG[part]], writes=[r_x[m][t]])
                        if hh == 1:
                            sq_emit(m, t)

                if hh == 0:
                    for (uid, ms) in out_units:
                        s = load_unit(uid)
                        for t in tiles:
                            out_jobs(s, uid, ms, t)
                        if post_unit is not None:
                            post_unit()
                else:
                    for p0 in (0, 2):
                        pair = out_units[p0:p0 + 2]
                        ss = [load_unit(uid) for (uid, ms) in pair]
                        for t in tiles:
                            for s, (uid, ms) in zip(ss, pair):
                                out_jobs(s, uid, ms, t)
                            if p0 == 2:
                                nxt.tile_done(t)
            nxt.flush()

        def mix_half(hf, nxt):
            mt = [2 * hf, 2 * hf + 1]
            base = 1024 * hf

            def lc(t):
                return TILES[t][0] - base

            sv = [load_unit(plan["v"][0]), load_unit(plan["v"][1])]

            def ln_tail(i, halves):
                p = i % 2
                sb = 8 + 4 * p
                emit(DVE, lambda: DVE.eng.bn_aggr(out=small[:, sb:sb + 2],
                                                   in_=stats[:, p, :, :].rearrange("p a b -> p (a b)")),
                     reads=[r_stats[p]], writes=[r_ln[p]])
                emit(ACT, lambda: ACT.eng.activation(out=small[:, sb + 2:sb + 3], in_=small[:, sb + 1:sb + 2],
                                                      func=AF.Sqrt, bias=small[:, 0:1], scale=1.0),
                     reads=[r_ln[p], r_small], writes=[r_ln[p]])
                emit(DVE, lambda: DVE.eng.reciprocal(out=small[:, sb + 3:sb + 4], in_=small[:, sb + 2:sb + 3]),
                     reads=[r_ln[p]], writes=[r_ln[p]])
                for uh in range(2):
                    gv, gvres = halves[uh]
                    emit(DVE, lambda gv=gv, uh=uh: DVE.eng.tensor_scalar(
                        out=vn_ap(i, uh * 512, 512), in0=gv[:, :], scalar1=small[:, sb:sb + 1],
                        scalar2=small[:, sb + 3:sb + 4], op0=ALU.subtract, op1=ALU.mult),
                        reads=[gvres, r_ln[p]], writes=[r_vn[i], r_regB_phase])

            pend = None
            for i in range(8):
                tcol = base + 128 * i
                tt = tcol // 512
                halves = []
                for uh in range(2):
                    s = sv[uh]
                    mms = []
                    for k in range(8):
                        mms.append((lambda b: banks[b][:, 0:512], h[:, k, tcol:tcol + 128],
                                    wslots[s][:, k * 512:(k + 1) * 512], k == 0, k == 7))
                    b = pe_job(mms, [r_wslot[s]] + [r_h[k][tt] for k in range(DC)])
                    gv, gvres = ringA.next()
                    emit(ACT, lambda b=b, gv=gv: ACT.eng.activation(out=gv[:, :], in_=banks[b][:, 0:512],
                                                                     func=AF.Gelu_apprx_tanh),
                         reads=[r_bank[b]], writes=[gvres])
                    emit(DVE, lambda gv=gv, uh=uh, i=i: DVE.eng.bn_stats(out=stats[:, i % 2, uh, :], in_=gv[:, :]),
                         reads=[gvres], writes=[r_stats[i % 2]])
                    halves.append((gv, gvres))
                if pend is not None:
                    ln_tail(*pend)
                pend = (i, halves)
            ln_tail(*pend)
            for j in range(8):
                uid = plan["cxb"][j]
                s = load_unit(uid)
                zs = j % 2
                zres = r_zb[zs]
                tl = ([HALO] if hf == 0 else []) + mt
                if hf == 1:
                    emit(DVE, lambda: DVE.eng.tensor_copy(out=zb_ap(zs, 0, 2), in_=zcar[:, j, :]),
                         reads=[r_zcar[j]], writes=[zres, r_regF_phase])
                for t in tl:
                    c0, n = TILES[t]
                    rh = [r_h[k][t] for k in range(DC)]
                    b = proj_job(s, uid, 0, 8, lambda k: h[:, k, c0:c0 + n], rh, n)
                    ct, ctres = ringA.next()
                    emit(ACT, lambda b=b, ct=ct: ACT.eng.activation(out=ct[:, 0:n], in_=banks[b][:, 0:n],
                                                                     func=AF.Copy),
                         reads=[r_bank[b]], writes=[ctres])
                    b = proj_job(s, uid, 1, 8, lambda k: h[:, k, c0:c0 + n], rh, n)
                    if t == HALO:
                        emit(DVE, lambda b=b, ct=ct: DVE.eng.scalar_tensor_tensor(
                            out=zb_ap(zs, 0, 2), in0=banks[b][:, 0:2], scalar=hm[:, 0:1], in1=ct[:, 0:2],
                            op0=ALU.mult, op1=ALU.mult),
                            reads=[r_bank[b], ctres, r_const], writes=[zres, r_regF_phase])
                        continue
                    l0 = lc(t)
                    emit(DVE, lambda b=b, ct=ct, l0=l0: DVE.eng.tensor_tensor(
                        out=zb_ap(zs, 2 + l0, 512), in0=banks[b][:, 0:512], in1=ct[:, :], op=ALU.mult),
                        reads=[r_bank[b], ctres], writes=[zres, r_regF_phase])
                    cv, cvres = ringD.next()
                    emit(ACT, lambda cv=cv, l0=l0: ACT.eng.activation(
                        out=cv[:, :], in_=zb_ap(zs, l0, 512), func=AF.Identity,
                        scale=vecs[:, V_CW + j:V_CW + j + 1]),
                        reads=[zres, r_const], writes=[cvres])
                    for kk in (1, 2):
                        emit(DVE, lambda cv=cv, l0=l0, kk=kk: DVE.eng.scalar_tensor_tensor(
                            out=cv[:, :], in0=zb_ap(zs, l0 + kk, 512),
                            scalar=vecs[:, V_CW + 8 * kk + j:V_CW + 8 * kk + j + 1], in1=cv[:, :],
                            op0=ALU.mult, op1=ALU.add),
                            reads=[zres, cvres, r_const], writes=[cvres])
                    b = proj_job(s, uid, 2, 8, lambda k: h[:, k, c0:c0 + n], rh, n)
                    ti = t - 2 * hf
                    emit(DVE, lambda b=b, cv=cv, l0=l0: DVE.eng.tensor_tensor(
                        out=za_ap(j, l0, 512), in0=banks[b][:, 0:512], in1=cv[:, :], op=ALU.mult),
                        reads=[r_bank[b], cvres], writes=[r_za[j][ti], r_regB_phase])
                if hf == 0:
                    emit(DVE, lambda: DVE.eng.tensor_copy(out=zcar[:, j, :], in_=zb_ap(zs, 1024, 2)),
                         reads=[zres], writes=[r_zcar[j]])
            for uu in range(2):
                uid = plan["u"][uu]
                s = load_unit(uid)
                for hl in range(4):
                    hd = 4 * uu + hl
                    ut = {}
                    for t in mt:
                        c0, n = TILES[t]
                        b = proj_job(s, uid, hl, 8, lambda k: h[:, k, c0:c0 + n], [r_h[k][t] for k in range(DC)], n)
                        u_ap, u_res = ringA.next()
                        emit(ACT, lambda b=b, u_ap=u_ap: ACT.eng.activation(
                            out=u_ap[:, :], in_=banks[b][:, 0:512], func=AF.Gelu_apprx_tanh),
                            reads=[r_bank[b]], writes=[u_res])
                        ut[t] = (u_ap, u_res)
                    for ti, t in enumerate(mt):
                        mms = []
                        for c in range(4):
                            i = ti * 4 + c
                            mms.append((lambda b, c=c: banks[b][:, c * 128:(c + 1) * 128],
                                        vn_ap(i, hd * 128, 128), wspT[:, hd, :], True, True))
                        b = pe_job(mms, [r_vn[ti * 4 + c] for c in range(4)] + [r_wspT])
                        sp_ap, sp_res = ringD.next()
                        for c in range(4):
                            emit(DVE, lambda b=b, c=c, sp_ap=sp_ap: DVE.eng.scalar_tensor_tensor(
                                out=sp_ap[:, c * 128:(c + 1) * 128], in0=banks[b][:, c * 128:(c + 1) * 128],
                                scalar=vecs[:, V_LNG + hd:V_LNG + hd + 1], in1=biast[:, hd, :],
                                op0=ALU.mult, op1=ALU.add),
                                reads=[r_bank[b], r_const, r_biast], writes=[sp_res])
                        u_ap, u_res = ut[t]
                        emit(DVE, lambda sp_ap=sp_ap, u_ap=u_ap, t=t: DVE.eng.tensor_tensor(
                            out=gb_ap(hd, lc(t), 512), in0=sp_ap[:, :], in1=u_ap[:, :], op=ALU.mult),
                            reads=[sp_res, u_res], writes=[r_gb[hd][ti], r_regB_phase])
            for m in range(8):
                uid = plan["p3"][m]
                s = load_unit(uid)
                for ti, t in enumerate(mt):
                    c0, n = TILES[t]
                    l0 = lc(t)
                    rh = [r_h[k][t] for k in range(DC)]
                    b = proj_job(s, uid, 0, 8, lambda k: h[:, k, c0:c0 + n], rh, n)
                    sa, sares = ringA.next()
                    emit(ACT, lambda b=b, sa=sa: ACT.eng.activation(out=sa[:, :], in_=banks[b][:, 0:512],
                                                                     func=AF.Sigmoid),
                         reads=[r_bank[b]], writes=[sares])
                    b = proj_job(s, uid, 1, 8, lambda k: h[:, k, c0:c0 + n], rh, n)
                    sb, sbres = ringA.next()
                    emit(ACT, lambda b=b, sb=sb: ACT.eng.activation(out=sb[:, :], in_=banks[b][:, 0:512],
                                                                     func=AF.Sigmoid),
                         reads=[r_bank[b]], writes=[sbres])
                    b = proj_job(s, uid, 2, 8, lambda k: za_ap(k, l0, 512), [r_za[k][ti] for k in range(8)], n)
                    t1, t1res = ringD.next()
                    emit(DVE, lambda b=b, sa=sa, t1=t1: DVE.eng.tensor_tensor(
                        out=t1[:, :], in0=banks[b][:, 0:512], in1=sa[:, :], op=ALU.mult),
                        reads=[r_bank[b], sares], writes=[t1res])
                    b = proj_job(s, uid, 3, 8, lambda k: gb_ap(k, l0, 512), [r_gb[k][ti] for k in range(8)], n)
                    t2, t2res = ringD.next()
                    emit(DVE, lambda b=b, sb=sb, t2=t2: DVE.eng.tensor_tensor(
                        out=t2[:, :], in0=banks[b][:, 0:512], in1=sb[:, :], op=ALU.mult),
                        reads=[r_bank[b], sbres], writes=[t2res])
                    emit(DVE, lambda t1=t1, t2=t2, l0=l0: DVE.eng.tensor_tensor(
                        out=mg_ap(m, l0, 512), in0=t1[:, :], in1=t2[:, :], op=ALU.add),
                        reads=[t1res, t2res], writes=[r_mg[m][ti], r_regB_phase] + r_vn)
            se = [load_unit(plan["mo"][q]) for q in range(2)]
            for ti, t in enumerate(mt):
                c0, n = TILES[t]
                l0 = lc(t)
                for q in range(2):
                    uid = plan["mo"][q]
                    s = se[q]
                    for mi in range(4):
                        m = 4 * q + mi
                        b = proj_job(s, uid, mi, 8, lambda k: mg_ap(k, l0, 512), [r_mg[k][ti] for k in range(8)], n)
                        emit(DVE, lambda b=b, m=m: DVE.eng.scalar_tensor_tensor(
                            out=x[:, m, c0:c0 + n], in0=banks[b][:, 0:n], scalar=v_gtw(1, m),
                            in1=x[:, m, c0:c0 + n], op0=ALU.mult, op1=ALU.add),
                            reads=[r_bank[b], r_x[m][t], r_adaG[1]], writes=[r_x[m][t]])
                        sq_emit(m, t)
                nxt.tile_done(t)

        def phase_fence(res_lists):
            pass

        ALL5 = [0, 1, 2, 3, 4]
        MAIN = [0, 1, 2, 3]

        def alias_fence():
            allres = ([r for row in r_sq for r in row] + [r for row in r_g for r in row] + r_vn +
                      [r for row in r_gb for r in row] + [r for row in r_za for r in row] +
                      [r for row in r_mg for r in row])
            toks = []
            wtoks = []
            for r in allres:
                toks.extend(r.r)
                if r.w is not None:
                    wtoks.append(r.w)
            best = {}
            for (c, v) in toks + wtoks:
                best[c] = max(best.get(c, 0), v)
            comp = [(c, v) for c, v in best.items()]
            for r in allres:
                r.r = list(comp)

        def spatial_setup():
            emit(POOL, lambda: POOL.eng.affine_select(
                out=wsp_f.rearrange("p (a b) -> p a b", a=8), in_=wsp_f.rearrange("p (a b) -> p a b", a=8),
                pattern=[[0, 8], [1, 128]], compare_op=ALU.is_ge, fill=0.0, base=0, channel_multiplier=-1),
                reads=[r_wsp], writes=[r_wsp])
            emit(DVE, lambda: DVE.eng.tensor_copy(out=wspT[:].rearrange("p a b -> p (a b)"), in_=wsp_f),
                 reads=[r_wsp], writes=[r_wspT])
            for q in range(2):
                mms = []
                for i in range(4):
                    hd = 4 * q + i
                    mms.append((lambda b, i=i: banks[b][:, i * 128:(i + 1) * 128], ones_f[:, :],
                                wsp_f[:, hd * 128:(hd + 1) * 128], True, True))
                b = pe_job(mms, [r_ones, r_wsp])
                for i in range(4):
                    hd = 4 * q + i
                    emit(DVE, lambda b=b, i=i, hd=hd: DVE.eng.scalar_tensor_tensor(
                        out=biast[:, hd, :], in0=banks[b][:, i * 128:(i + 1) * 128],
                        scalar=vecs[:, V_LNB + hd:V_LNB + hd + 1], in1=bsp_b[:, hd * 128:(hd + 1) * 128],
                        op0=ALU.mult, op1=ALU.add),
                        reads=[r_bank[b], r_const, r_bspb], writes=[r_biast])
            r_regF_phase.r = list(r_wsp.r) + list(r_bspb.r)
            r_regF_phase.w = r_wsp.w

        n1 = NormPipe(0)
        load_x(0)
        for j in range(DC):
            sq_emit(j, 0)
        n1._ab(0, 6)
        ada_units(0, [0, 1])
        load_x(1, after=ada_tok[(0, 1)])
        for j in range(DC):
            sq_emit(j, 1)
        n1._ab(1, 7)
        spatial_setup()
        ada_units(0, [2])
        load_x(2, after=ada_tok[(0, 2)])
        ada_units(0, [3])
        load_x(3, after=ada_tok[(0, 3)])
        load_x(4)
        pro_rr = {}
        for t, (o, w) in zip((2, 3, 4), ((0, 512), (512, 512), (1024, 2))):
            for j in range(DC):
                sq_emit(j, t)
            dres = Res()
            dst = regF[:, o:o + 512] if w == 512 else regF[:, o:o + 2]
            n1._ab_sb(t, dst, dres, [r_regF_phase])
            pro_rr[t] = (dst, dres)
        n1._c(0, banks[6], r_bank[6])
        n1._c(1, banks[7], r_bank[7])

        def pro_pre_tile(t):
            if t + 2 <= 4:
                dst, dres = pro_rr[t + 2]
                n1._c(t + 2, dst, dres, extra_r=[r_regF_phase])

        ada_q = [(0, 4), (0, 5)] + [(1, q) for q in range(6)] + [(2, q) for q in range(6)]

        def pop_ada():
            if ada_q:
                p, q = ada_q.pop(0)
                ada_units(p, [q])

        n2 = NormPipe(1)
        ffn("ffn1", 0, ALL5, n2, pre_tile=pro_pre_tile, post_unit=pop_ada)
        assert not ada_q
        alias_fence()
        n3 = NormPipe(2)
        mix_half(0, n3)
        alias_fence()
        mix_half(1, n3)
        n3.flush()
        alias_fence()
        nf = NormPipe(None, final=True)
        ffn("ffn2", 2, MAIN, nf)
        streams["sp"].append({"m": "wait_ge", "a": (c_st.sem, c_st.val), "kw": {}, "inc": None})

        block = E(nc.Block())

        def replay(eng, items):
            for it in items:
                ins = getattr(eng, it["m"])(*it["a"], **it["kw"])
                if it["inc"] is not None:
                    ins.then_inc(*it["inc"])

        @block.tensor
        def _(e):
            replay(e, streams["pe"])

        @block.scalar
        def _(e):
            replay(e, streams["act"])

        @block.vector
        def _(e):
            replay(e, streams["dve"])

        @block.gpsimd
        def _(e):
            replay(e, streams["pool"])

        @block.sync
        def _(e):
            replay(e, streams["sp"])
    return nc


def kernel(x, c, w_ada, b_ada, g_ffn1, w_ffn1_in, w_ffn1_out, g_mix, w_mix_in, conv_w, ln_v_g, ln_v_b,
           w_spatial, b_spatial, w_a_out, w_b_out, w_mix_out, g_ffn2, w_ffn2_in, w_ffn2_out, g_final):
    f = lambda a: np.asarray(a, dtype=np.float32)
    x = f(x)
    W = {k: f(v) for k, v in dict(w_ada=w_ada, w_ffn1_in=w_ffn1_in, w_ffn1_out=w_ffn1_out, w_mix_in=w_mix_in,
                                  w_a_out=w_a_out, w_b_out=w_b_out, w_mix_out=w_mix_out, w_ffn2_in=w_ffn2_in,
                                  w_ffn2_out=w_ffn2_out).items()}
    stream, units, plan = build_weight_stream(W)
    fm = lambda v: f(v).reshape(8, 128).T
    wsp = np.ascontiguousarray(f(w_spatial).transpose(2, 0, 1))
    bsp = np.ascontiguousarray(np.broadcast_to(f(b_spatial).reshape(1, 1024), (128, 1024)))
    cw = f(conv_w)
    in_maps = []
    per_b = NCORES // x.shape[0]
    for core in range(NCORES):
        b = core // per_b
        s0 = (core % per_b) * TOK
        xs = x[b, s0:s0 + TOK, :]
        xt = np.zeros((128, DC, TH), np.float32)
        xt[:, :, :TOK] = xs.reshape(TOK, DC, 128).transpose(2, 1, 0)
        hmv = 0.0
        if s0 > 0:
            xt[:, :, TOK:TH] = x[b, s0 - 2:s0, :].reshape(2, DC, 128).transpose(2, 1, 0)
            hmv = 1.0
        vecs = np.zeros((128, NV), np.float32)
        vecs[:, V_G1:V_G1 + 8] = fm(g_ffn1)
        vecs[:, V_GM:V_GM + 8] = fm(g_mix)
        vecs[:, V_G2:V_G2 + 8] = fm(g_ffn2)
        vecs[:, V_GF:V_GF + 8] = fm(g_final)
        vecs[:, V_LNG:V_LNG + 8] = fm(ln_v_g)
        vecs[:, V_LNB:V_LNB + 8] = fm(ln_v_b)
        for k in range(3):
            vecs[:, V_CW + 8 * k:V_CW + 8 * k + 8] = fm(cw[k])
        vecs[:, V_BADA:V_BADA + 72] = f(b_ada).reshape(72, 128).T
        vecs[:, V_C:V_C + 8] = fm(f(c)[b])
        in_maps.append({"xT": xt, "vecs": vecs, "wsp": wsp, "bsp": bsp,
                        "hmask": np.full((128, 1), hmv, np.float32), "wst": stream})
    nc = build_program(units, plan, stream.shape[0])
    res = run_bass_kernel_spmd(nc, in_maps, core_ids=list(range(NCORES)))
    out = np.empty((x.shape[0], SEQ, D), np.float32)
    for core in range(NCORES):
        b = core // per_b
        s0 = (core % per_b) * TOK
        yt = np.asarray(res.results[core]["yT"])
        out[b, s0:s0 + TOK, :] = yt.transpose(2, 1, 0).reshape(TOK, D)
    return out
```

```python
import numpy as np
from contextlib import ExitStack
import concourse.bass as bass
import concourse.mybir as mybir
from concourse.bass_utils import run_bass_kernel_spmd

F32 = mybir.dt.float32
BF16 = mybir.dt.bfloat16
AF = mybir.ActivationFunctionType
ALU = mybir.AluOpType

D = 1024
DC = 8
TOK = 2048
TH = 2050
DFF = 2816
FC = 22
FH = 11
EPS = 1e-6
NCORES = 8
SEQ = 8192
TILES = [(0, 512), (512, 512), (1024, 512), (1536, 512), (2048, 2)]
HALO = 4
NSLOT = 4

V_G1, V_GM, V_G2, V_GF, V_LNG, V_LNB, V_CW, V_BADA, V_C = 0, 8, 16, 24, 32, 40, 48, 72, 144
NV = 152


class UnitMeta:
    def __init__(self, off, kc, w):
        self.off, self.kc, self.w = off, kc, w
        self.L = kc * w


def build_weight_stream(W):
    units = []
    bufs = []
    pos = [0]

    def add_unit(mats):
        kc = len(mats[0][1])
        blocks = []
        for (M, rcs, c0) in mats:
            rows = np.concatenate([np.arange(r * 128, r * 128 + 128) for r in rcs])
            blk = M[rows, c0:c0 + 128].reshape(kc, 128, 128).transpose(1, 0, 2)
            blocks.append(blk)
        u = np.concatenate(blocks, axis=2)
        w = u.shape[2]
        units.append(UnitMeta(pos[0], kc, w))
        bufs.append(np.ascontiguousarray(u, dtype=np.float32).reshape(-1))
        pos[0] += 128 * kc * w
        return len(units) - 1

    K8 = list(range(8))
    plan = {}
    plan["ada"] = []
    for part in range(3):
        us = []
        for q in range(6):
            c0 = part * 3072 + q * 512
            us.append(add_unit([(W["w_ada"], K8, c0 + 128 * i) for i in range(4)]))
        plan["ada"].append(us)
    for name, win, wout in (("ffn1", "w_ffn1_in", "w_ffn1_out"), ("ffn2", "w_ffn2_in", "w_ffn2_out")):
        halves = []
        for hh in range(2):
            seq = []
            for j in range(hh * FH, hh * FH + FH):
                seq.append(("a", j, (W[win], K8, j * 128)))
                seq.append(("b", j, (W[win], K8, DFF + j * 128)))
            in_units = []
            for i in range(0, len(seq), 4):
                grp = seq[i:i + 4]
                uid = add_unit([g[2] for g in grp])
                in_units.append((uid, [(g[0], g[1]) for g in grp]))
            out_units = []
            rcs = list(range(hh * FH, hh * FH + FH))
            for m0 in range(0, 8, 2):
                uid = add_unit([(W[wout], rcs, m * 128) for m in (m0, m0 + 1)])
                out_units.append((uid, [m0, m0 + 1]))
            halves.append((in_units, out_units))
        plan[name] = halves
    wm = W["w_mix_in"]
    O_GA, O_GB, O_B, O_C, O_X, O_U, O_V = [i * 1024 for i in range(7)]
    plan["v"] = [add_unit([(wm, K8, O_V + uh * 512 + 128 * i) for i in range(4)]) for uh in range(2)]
    plan["u"] = [add_unit([(wm, K8, O_U + uu * 512 + 128 * i) for i in range(4)]) for uu in range(2)]
    plan["cxb"] = [add_unit([(wm, K8, O_C + j * 128), (wm, K8, O_X + j * 128), (wm, K8, O_B + j * 128)])
                   for j in range(8)]
    plan["p3"] = [add_unit([(wm, K8, O_GA + m * 128), (wm, K8, O_GB + m * 128),
                            (W["w_a_out"], K8, m * 128), (W["w_b_out"], K8, m * 128)]) for m in range(8)]
    plan["mo"] = [add_unit([(W["w_mix_out"], K8, (4 * q + i) * 128) for i in range(4)]) for q in range(2)]
    stream = np.concatenate(bufs)
    return stream, units, plan


class Ctr:
    def __init__(self, sem, step):
        self.sem, self.step, self.val = sem, step, 0


class Res:
    __slots__ = ("w", "r")

    def __init__(self):
        self.w = None
        self.r = []


class Q:
    def __init__(self, eng, ctr=None):
        self.eng, self.ctr, self.known = eng, ctr, {}

    def wait(self, ctr, val):
        if self.known.get(ctr, 0) >= val:
            return
        self.eng.wait_ge(ctr.sem, val)
        self.known[ctr] = val


def _deps(q, reads, writes):
    deps = {}
    for r in reads:
        if r.w is not None:
            c, v = r.w
            deps[c] = max(deps.get(c, 0), v)
    for w in writes:
        if w.w is not None:
            c, v = w.w
            deps[c] = max(deps.get(c, 0), v)
        for (c, v) in w.r:
            deps[c] = max(deps.get(c, 0), v)
    for c, v in deps.items():
        q.wait(c, v)


def emit(q, fn, reads=(), writes=(), ctr=None):
    _deps(q, reads, writes)
    ins = fn()
    c = ctr if ctr is not None else q.ctr
    c.val += c.step
    ins.then_inc(c.sem, c.step)
    tok = (c, c.val)
    for r in reads:
        r.r.append(tok)
    for w in writes:
        w.w = tok
        w.r = []
    return tok


class Ring:
    def __init__(self, aps):
        self.aps = aps
        self.res = [Res() for _ in aps]
        self.i = 0

    def next(self):
        k = self.i % len(self.aps)
        self.i += 1
        return self.aps[k], self.res[k]


def build_program(units, plan, total_w):
    nc = bass.Bass("TRN2", target_bir_lowering=False)
    xT_d = nc.dram_tensor("xT", [128, DC, TH], F32, kind="ExternalInput").ap()
    vecs_d = nc.dram_tensor("vecs", [128, NV], F32, kind="ExternalInput").ap()
    wsp_d = nc.dram_tensor("wsp", [128, 8, 128], F32, kind="ExternalInput").ap()
    bsp_d = nc.dram_tensor("bsp", [128, 1024], F32, kind="ExternalInput").ap()
    hm_d = nc.dram_tensor("hmask", [128, 1], F32, kind="ExternalInput").ap()
    wst_d = nc.dram_tensor("wst", [total_w], F32, kind="ExternalInput").ap()
    yT_d = nc.dram_tensor("yT", [128, DC, TOK], F32, kind="ExternalOutput").ap()

    with ExitStack() as es:
        E = es.enter_context
        x = E(nc.sbuf_tensor("x", [128, DC, TH], F32))
        h = E(nc.sbuf_tensor("h", [128, DC, TH], BF16))
        regB = E(nc.sbuf_tensor("regB", [128, 24576], BF16))
        regF = E(nc.sbuf_tensor("regF", [128, 2 * 1026], F32))
        wslots = [E(nc.sbuf_tensor(f"wslot{i}", [128, 4096], BF16)) for i in range(NSLOT)]
        tA = [E(nc.sbuf_tensor(f"tA{i}", [128, 512], F32)) for i in range(4)]
        tD = [E(nc.sbuf_tensor(f"tD{i}", [128, 512], F32)) for i in range(3)]
        vecs = E(nc.sbuf_tensor("vecs_sb", [128, NV], F32))
        adaT = E(nc.sbuf_tensor("adaT", [128, 72], F32))
        der = E(nc.sbuf_tensor("der", [128, 48], F32))
        small = E(nc.sbuf_tensor("small", [128, 64], F32))
        hm = E(nc.sbuf_tensor("hm", [128, 1], F32))
        sc_bf = E(nc.sbuf_tensor("sc_bf", [128, 8, 2], BF16))
        ones_bf = E(nc.sbuf_tensor("ones_bf", [128, 128], BF16))
        ones_f = E(nc.sbuf_tensor("ones_f", [128, 128], F32))
        wspT = E(nc.sbuf_tensor("wspT", [128, 8, 128], BF16))
        biast = E(nc.sbuf_tensor("biast", [128, 8, 128], F32))
        zcar = E(nc.sbuf_tensor("zcar", [128, 8, 2], F32))
        stats = E(nc.sbuf_tensor("stats", [128, 2, 2, 6], F32))
        banks = [E(nc.psum_tensor(f"ps{i}", [128, 512], F32)) for i in range(8)]

        s_pe = E(nc.semaphore("s_pe"))
        s_act = E(nc.semaphore("s_act"))
        s_dve = E(nc.semaphore("s_dve"))
        s_pool = E(nc.semaphore("s_pool"))
        s_ld = E(nc.semaphore("s_ld"))
        s_x = [E(nc.semaphore(f"s_x{i}")) for i in range(DC)]
        s_st = E(nc.semaphore("s_st"))
        s_w = [E(nc.semaphore(f"s_w{i}")) for i in range(NSLOT)]

        c_pe, c_act, c_dve, c_pool = Ctr(s_pe, 1), Ctr(s_act, 1), Ctr(s_dve, 1), Ctr(s_pool, 1)
        c_ld, c_st = Ctr(s_ld, 16), Ctr(s_st, 16)
        c_x = [Ctr(s, 16) for s in s_x]
        c_w = [Ctr(s, 16) for s in s_w]

        streams = {"pe": [], "act": [], "dve": [], "pool": [], "sp": []}

        class Rec:
            def __init__(self, name):
                self.name = name

            def __getattr__(self, meth):
                def call(*a, **kw):
                    item = {"m": meth, "a": a, "kw": kw, "inc": None}
                    streams[self.name].append(item)

                    class H:
                        def then_inc(_s, sem, n):
                            item["inc"] = (sem, n)
                            return _s
                    return H()
                return call

        PE, ACT, DVE, POOL, SP = (Q(Rec("pe"), c_pe), Q(Rec("act"), c_act), Q(Rec("dve"), c_dve),
                                  Q(Rec("pool"), c_pool), Q(Rec("sp"), None))

        r_x = [[Res() for _ in TILES] for _ in range(DC)]
        r_h = [[Res() for _ in TILES] for _ in range(DC)]
        r_sq = [[Res() for _ in TILES] for _ in range(DC)]
        r_g = [[Res() for _ in TILES] for _ in range(FH)]
        r_bank = [Res() for _ in range(8)]
        r_wslot = [Res() for _ in range(NSLOT)]
        r_const = Res()
        r_sc = Res()
        r_ada = [Res() for _ in range(3)]
        r_der = [Res() for _ in range(3)]
        r_wsp = Res()
        r_bspb = Res()
        r_wspT = Res()
        r_biast = Res()
        r_ones = Res()
        r_small = Res()
        r_stats = [Res(), Res()]
        r_ln = [Res(), Res()]
        r_vn = [Res() for _ in range(8)]
        r_gb = [[Res(), Res()] for _ in range(8)]
        r_za = [[Res(), Res()] for _ in range(8)]
        r_mg = [[Res(), Res()] for _ in range(8)]
        r_zb = [Res(), Res()]
        r_zcar = [Res() for _ in range(8)]
        ringA = Ring([t[:] for t in tA])
        ringD = Ring([t[:] for t in tD])

        def g_ap(jj, c0, n):
            return regB[:, jj * TH + c0: jj * TH + c0 + n]

        def sq_ap(j, c0, n):
            return regB[:, j * TH + c0: j * TH + c0 + n]

        def vn_ap(i, c0, n):
            return regB[:, i * 1024 + c0: i * 1024 + c0 + n]

        def mg_ap(m, c0, n):
            return regB[:, m * 1024 + c0: m * 1024 + c0 + n]

        def gb_ap(hd, c0, n):
            return regB[:, 8192 + hd * 1024 + c0: 8192 + hd * 1024 + c0 + n]

        def za_ap(j, c0, n):
            return regB[:, 16384 + j * 1024 + c0: 16384 + j * 1024 + c0 + n]

        def zb_ap(s, c0, n):
            return regF[:, s * 1026 + c0: s * 1026 + c0 + n]

        wsp_f = regF[:, 0:1024]
        bsp_b = regF[:, 1024:2048]

        r_regB_phase = Res()
        r_regF_phase = Res()

        job_ctr = [0]
        wstate = {"next_slot": 0, "unit_slot": {}, "uses": 0}

        def load_unit(uid):
            s = wstate["next_slot"] % NSLOT
            wstate["next_slot"] += 1
            um = units[uid]
            src = wst_d[um.off: um.off + 128 * um.L].rearrange("(p k w) -> p k w", p=128, k=um.kc)
            dst = wslots[s][:, 0:um.L].rearrange("p (k w) -> p k w", k=um.kc)
            wstate["last_tok"] = emit(POOL, lambda: POOL.eng.dma_start(out=dst, in_=src), reads=[],
                                      writes=[r_wslot[s]], ctr=c_w[s])
            return s

        def wchunk(s, uid, k, mi):
            um = units[uid]
            o = k * um.w + mi * 128
            return wslots[s][:, o:o + 128]

        def pe_job(mms, reads, bank=None):
            if bank is None:
                b = job_ctr[0] % 6
                job_ctr[0] += 1
            else:
                b = bank
            _deps(PE, reads, [r_bank[b]])
            last = None
            for (o, l, r, st, sp) in mms:
                last = PE.eng.matmul(o(b), l, r, start=st, stop=sp)
            c_pe.val += 1
            last.then_inc(c_pe.sem, 1)
            tok = (c_pe, c_pe.val)
            for r in reads:
                r.r.append(tok)
            r_bank[b].w = tok
            r_bank[b].r = []
            return b

        def proj_job(s, uid, mi, kc, rhs_fn, rhs_res, n, extra_reads=()):
            mms = []
            for k in range(kc):
                mms.append((lambda b, n=n: banks[b][:, 0:n], wchunk(s, uid, k, mi), rhs_fn(k), k == 0, k == kc - 1))
            return pe_job(mms, [r_wslot[s]] + list(rhs_res) + list(extra_reads))

        emit(SP, lambda: SP.eng.dma_start(out=wsp_f.rearrange("p (a b) -> p a b", a=8), in_=wsp_d[:, :, :]),
             writes=[r_wsp], ctr=c_ld)
        emit(SP, lambda: SP.eng.dma_start(out=bsp_b, in_=bsp_d[:, :]), writes=[r_bspb], ctr=c_ld)
        emit(SP, lambda: SP.eng.dma_start(out=vecs[:], in_=vecs_d[:, :]), writes=[r_const], ctr=c_ld)
        emit(SP, lambda: SP.eng.dma_start(out=hm[:], in_=hm_d[:, :]), writes=[r_const], ctr=c_ld)
        tot = (c_ld, c_ld.val)
        for r in [r_const, r_wsp, r_bspb]:
            r.w = tot
        def load_x(t, after=None):
            c0, n = TILES[t]
            if after is not None:
                SP.wait(*after)
            emit(SP, lambda: SP.eng.dma_start(out=x[:, :, c0:c0 + n], in_=xT_d[:, :, c0:c0 + n]),
                 writes=[r_x[j][t] for j in range(DC)], ctr=c_x[t])

        emit(DVE, lambda: DVE.eng.memset(ones_bf[:], 1.0), writes=[r_ones])
        emit(DVE, lambda: DVE.eng.memset(ones_f[:], 1.0), writes=[r_ones])
        emit(DVE, lambda: DVE.eng.memset(small[:, 0:1], EPS), writes=[r_small])
        for dup in range(2):
            emit(ACT, lambda dup=dup: ACT.eng.activation(out=sc_bf[:, :, dup], in_=vecs[:, V_C:V_C + 8], func=AF.Silu),
                 reads=[r_const], writes=[r_sc])
        G_COL = [V_G1, V_GM, V_G2]

        ada_tok = {}
        r_adaS = [Res() for _ in range(3)]
        r_adaG = [Res() for _ in range(3)]

        def ada_units(part, qs):
            for q in qs:
                uid = plan["ada"][part][q]
                s = load_unit(uid)
                ada_tok[(part, q)] = wstate["last_tok"]
                mms = []
                for i in range(4):
                    for k in range(8):
                        mms.append((lambda b, i=i: banks[b][:, 2 * i:2 * i + 2], wchunk(s, uid, k, i),
                                    sc_bf[:, k, :], k == 0, k == 7))
                b = pe_job(mms, [r_wslot[s], r_sc])
                c0 = part * 24 + q * 4
                rr = r_adaS[part] if q < 4 else r_adaG[part]
                emit(DVE, lambda b=b, c0=c0: DVE.eng.tensor_tensor(
                    out=adaT[:, c0:c0 + 4], in0=banks[b][:, 0:8].rearrange("p (a b) -> p a b", b=2)[:, :, 0],
                    in1=vecs[:, V_BADA + c0:V_BADA + c0 + 4], op=ALU.add),
                    reads=[r_bank[b], r_const], writes=[rr])
            a0 = part * 24
            if 3 in qs:
                emit(DVE, lambda: DVE.eng.scalar_tensor_tensor(
                    out=der[:, part * 16:part * 16 + 8], in0=adaT[:, a0 + 8:a0 + 16], scalar=1.0,
                    in1=vecs[:, G_COL[part]:G_COL[part] + 8], op0=ALU.add, op1=ALU.mult),
                    reads=[r_adaS[part], r_const], writes=[r_adaS[part]])
            if 5 in qs:
                emit(DVE, lambda: DVE.eng.tensor_scalar(
                    out=der[:, part * 16 + 8:part * 16 + 16], in0=adaT[:, a0 + 16:a0 + 24],
                    scalar1=(1.0 if part == 1 else 0.5), scalar2=None, op0=ALU.mult),
                    reads=[r_adaG[part]], writes=[r_adaG[part]])

        def v_scale(part, j):
            return der[:, part * 16 + j: part * 16 + j + 1]

        def v_gtw(part, j):
            return der[:, part * 16 + 8 + j: part * 16 + 8 + j + 1]

        def v_shift(part, j):
            return adaT[:, part * 24 + j: part * 24 + j + 1]

        def sq_emit(j, t):
            c0, n = TILES[t]
            emit(ACT, lambda: ACT.eng.activation(out=h[:, j, c0:c0 + n], in_=x[:, j, c0:c0 + n], func=AF.Square),
                 reads=[r_x[j][t]], writes=[r_h[j][t]])

        class NormPipe:
            def __init__(self, part, final=False):
                self.part, self.final, self.prev, self.cnt = part, final, None, 0

            def tile_done(self, t):
                if self.prev is not None:
                    self._emit(self.prev)
                self.prev = t

            def flush(self):
                if self.prev is not None:
                    self._emit(self.prev)
                    self.prev = None

            def _emit(self, t):
                nb = 6 + (self.cnt % 2)
                self.cnt += 1
                self._ab(t, nb)
                self._c(t, banks[nb], r_bank[nb])

            def _ss(self, t, bank):
                c0, n = TILES[t]
                mms = []
                for j in range(DC):
                    mms.append((lambda b: banks[b][:, 0:n], ones_bf[:, :], h[:, j, c0:c0 + n], j == 0, j == DC - 1))
                return pe_job(mms, [r_ones] + [r_h[j][t] for j in range(DC)], bank=bank)

            def _ab(self, t, nb):
                c0, n = TILES[t]
                self._ss(t, nb)
                pb = banks[nb]
                emit(ACT, lambda: ACT.eng.activation(out=pb[:, 0:n], in_=pb[:, 0:n], func=AF.Sqrt,
                                                      bias=small[:, 0:1], scale=1.0 / D),
                     reads=[r_bank[nb], r_small], writes=[r_bank[nb]])
                emit(DVE, lambda: DVE.eng.reciprocal(out=pb[:, 0:n], in_=pb[:, 0:n]),
                     reads=[r_bank[nb]], writes=[r_bank[nb]])

            def _ab_sb(self, t, dst, dres, extra_w):
                c0, n = TILES[t]
                b = self._ss(t, None)
                pb = banks[b]
                emit(ACT, lambda: ACT.eng.activation(out=pb[:, 0:n], in_=pb[:, 0:n], func=AF.Sqrt,
                                                      bias=small[:, 0:1], scale=1.0 / D),
                     reads=[r_bank[b], r_small], writes=[r_bank[b]])
                emit(DVE, lambda: DVE.eng.reciprocal(out=dst[:, 0:n], in_=pb[:, 0:n]),
                     reads=[r_bank[b]], writes=[dres] + list(extra_w))

            def _c(self, t, rr, rres, extra_r=()):
                part, final = self.part, self.final
                c0, n = TILES[t]
                for j in range(DC):
                    if final:
                        emit(DVE, lambda j=j: DVE.eng.scalar_tensor_tensor(
                            out=x[:, j, c0:c0 + n], in0=x[:, j, c0:c0 + n], scalar=vecs[:, V_GF + j:V_GF + j + 1],
                            in1=rr[:, 0:n], op0=ALU.mult, op1=ALU.mult),
                            reads=[r_x[j][t], rres, r_const], writes=[r_x[j][t]])
                        if j % 2 == 1:
                            emit(SP, lambda j=j: SP.eng.dma_start(out=yT_d[:, j - 1:j + 1, c0:c0 + n],
                                                                   in_=x[:, j - 1:j + 1, c0:c0 + n]),
                                 reads=[r_x[j - 1][t], r_x[j][t]], ctr=c_st)
                    else:
                        ht, htres = ringD.next()
                        emit(DVE, lambda j=j, ht=ht: DVE.eng.scalar_tensor_tensor(
                            out=ht[:, 0:n], in0=x[:, j, c0:c0 + n], scalar=v_scale(part, j),
                            in1=rr[:, 0:n], op0=ALU.mult, op1=ALU.mult),
                            reads=[r_x[j][t], rres, r_adaS[part]] + list(extra_r), writes=[htres])
                        emit(ACT, lambda j=j, ht=ht: ACT.eng.activation(
                            out=h[:, j, c0:c0 + n], in_=ht[:, 0:n], func=AF.Identity,
                            bias=v_shift(part, j), scale=1.0),
                            reads=[htres, r_adaS[part]], writes=[r_h[j][t]])

        def ffn(name, part, tiles, nxt, pre_tile=None, post_unit=None):
            first = True
            for hh in range(2):
                in_units, out_units = plan[name][hh]
                for (uid, chunks) in in_units:
                    s = load_unit(uid)
                    for t in tiles:
                        c0, n = TILES[t]
                        st_ap = None
                        for mi, (kind, j) in enumerate(chunks):
                            b = proj_job(s, uid, mi, 8, lambda k: h[:, k, c0:c0 + n],
                                         [r_h[k][t] for k in range(DC)], n)
                            if kind == "a":
                                st_ap, st_res = ringA.next()
                                emit(ACT, lambda b=b, st_ap=st_ap: ACT.eng.activation(
                                    out=st_ap[:, 0:n], in_=banks[b][:, 0:n], func=AF.Silu),
                                    reads=[r_bank[b]], writes=[st_res])
                            else:
                                jj = j - hh * FH
                                emit(DVE, lambda b=b, st_ap=st_ap, jj=jj: DVE.eng.tensor_tensor(
                                    out=g_ap(jj, c0, n), in0=banks[b][:, 0:n], in1=st_ap[:, 0:n], op=ALU.mult),
                                    reads=[r_bank[b], st_res], writes=[r_g[jj][t]])
                        if first and pre_tile is not None:
                            pre_tile(t)
                    first = False
                    if post_unit is not None:
                        post_unit()

                def out_jobs(s, uid, ms, t):
                    c0, n = TILES[t]
                    for mi, m in enumerate(ms):
                        b = proj_job(s, uid, mi, FH, lambda k: g_ap(k, c0, n),
                                     [r_g[k][t] for k in range(FH)], n)
                        emit(DVE, lambda b=b, m=m: DVE.eng.scalar_tensor_tensor(
                            out=x[:, m, c0:c0 + n], in0=banks[b][:, 0:n], scalar=v_gtw(part, m),
                            in1=x[:, m, c0:c0 + n], op0=ALU.mult, op1=ALU.add),
                            reads=[r_bank[b], r_x[m][t], r_adaG[part]], writes=[r_x[m][t]])
                        if hh == 1:
                            sq_emit(m, t)

                if hh == 0:
                    for (uid, ms) in out_units:
                        s = load_unit(uid)
                        for t in tiles:
                            out_jobs(s, uid, ms, t)
                        if post_unit is not None:
                            post_unit()
                else:
                    for p0 in (0, 2):
                        pair = out_units[p0:p0 + 2]
                        ss = [load_unit(uid) for (uid, ms) in pair]
                        for t in tiles:
                            for s, (uid, ms) in zip(ss, pair):
                                out_jobs(s, uid, ms, t)
                            if p0 == 2:
                                nxt.tile_done(t)
            nxt.flush()

        def mix_half(hf, nxt):
            mt = [2 * hf, 2 * hf + 1]
            base = 1024 * hf

            def lc(t):
                return TILES[t][0] - base

            sv = [load_unit(plan["v"][0]), load_unit(plan["v"][1])]

            def ln_tail(i, halves):
                p = i % 2
                sb = 8 + 4 * p
                emit(DVE, lambda: DVE.eng.bn_aggr(out=small[:, sb:sb + 2],
                                                   in_=stats[:, p, :, :].rearrange("p a b -> p (a b)")),
                     reads=[r_stats[p]], writes=[r_ln[p]])
                emit(ACT, lambda: ACT.eng.activation(out=small[:, sb + 2:sb + 3], in_=small[:, sb + 1:sb + 2],
                                                      func=AF.Sqrt, bias=small[:, 0:1], scale=1.0),
                     reads=[r_ln[p], r_small], writes=[r_ln[p]])
                emit(DVE, lambda: DVE.eng.reciprocal(out=small[:, sb + 3:sb + 4], in_=small[:, sb + 2:sb + 3]),
                     reads=[r_ln[p]], writes=[r_ln[p]])
                for uh in range(2):
                    gv, gvres = halves[uh]
                    emit(DVE, lambda gv=gv, uh=uh: DVE.eng.tensor_scalar(
                        out=vn_ap(i, uh * 512, 512), in0=gv[:, :], scalar1=small[:, sb:sb + 1],
                        scalar2=small[:, sb + 3:sb + 4], op0=ALU.subtract, op1=ALU.mult),
                        reads=[gvres, r_ln[p]], writes=[r_vn[i]])

            pend = None
            for i in range(8):
                tcol = base + 128 * i
                tt = tcol // 512
                halves = []
                for uh in range(2):
                    s = sv[uh]
                    mms = []
                    for k in range(8):
                        mms.append((lambda b: banks[b][:, 0:512], h[:, k, tcol:tcol + 128],
                                    wslots[s][:, k * 512:(k + 1) * 512], k == 0, k == 7))
                    b = pe_job(mms, [r_wslot[s]] + [r_h[k][tt] for k in range(DC)])
                    gv, gvres = ringA.next()
                    emit(ACT, lambda b=b, gv=gv: ACT.eng.activation(out=gv[:, :], in_=banks[b][:, 0:512],
                                                                     func=AF.Gelu_apprx_tanh),
                         reads=[r_bank[b]], writes=[gvres])
                    emit(DVE, lambda gv=gv, uh=uh, i=i: DVE.eng.bn_stats(out=stats[:, i % 2, uh, :], in_=gv[:, :]),
                         reads=[gvres], writes=[r_stats[i % 2]])
                    halves.append((gv, gvres))
                if pend is not None:
                    ln_tail(*pend)
                pend = (i, halves)
            ln_tail(*pend)
            for j in range(8):
                uid = plan["cxb"][j]
                s = load_unit(uid)
                zs = j % 2
                zres = r_zb[zs]
                tl = ([HALO] if hf == 0 else []) + mt
                if hf == 1:
                    emit(DVE, lambda: DVE.eng.tensor_copy(out=zb_ap(zs, 0, 2), in_=zcar[:, j, :]),
                         reads=[r_zcar[j]], writes=[zres])
                for t in tl:
                    c0, n = TILES[t]
                    rh = [r_h[k][t] for k in range(DC)]
                    b = proj_job(s, uid, 0, 8, lambda k: h[:, k, c0:c0 + n], rh, n)
                    ct, ctres = ringA.next()
                    emit(ACT, lambda b=b, ct=ct: ACT.eng.activation(out=ct[:, 0:n], in_=banks[b][:, 0:n],
                                                                     func=AF.Copy),
                         reads=[r_bank[b]], writes=[ctres])
                    b = proj_job(s, uid, 1, 8, lambda k: h[:, k, c0:c0 + n], rh, n)
                    if t == HALO:
                        emit(DVE, lambda b=b, ct=ct: DVE.eng.scalar_tensor_tensor(
                            out=zb_ap(zs, 0, 2), in0=banks[b][:, 0:2], scalar=hm[:, 0:1], in1=ct[:, 0:2],
                            op0=ALU.mult, op1=ALU.mult),
                            reads=[r_bank[b], ctres, r_const], writes=[zres])
                        continue
                    l0 = lc(t)
                    emit(DVE, lambda b=b, ct=ct, l0=l0: DVE.eng.tensor_tensor(
                        out=zb_ap(zs, 2 + l0, 512), in0=banks[b][:, 0:512], in1=ct[:, :], op=ALU.mult),
                        reads=[r_bank[b], ctres], writes=[zres])
                    cv, cvres = ringD.next()
                    emit(ACT, lambda cv=cv, l0=l0: ACT.eng.activation(
                        out=cv[:, :], in_=zb_ap(zs, l0, 512), func=AF.Identity,
                        scale=vecs[:, V_CW + j:V_CW + j + 1]),
                        reads=[zres, r_const], writes=[cvres])
                    for kk in (1, 2):
                        emit(DVE, lambda cv=cv, l0=l0, kk=kk: DVE.eng.scalar_tensor_tensor(
                            out=cv[:, :], in0=zb_ap(zs, l0 + kk, 512),
                            scalar=vecs[:, V_CW + 8 * kk + j:V_CW + 8 * kk + j + 1], in1=cv[:, :],
                            op0=ALU.mult, op1=ALU.add),
                            reads=[zres, cvres, r_const], writes=[cvres])
                    b = proj_job(s, uid, 2, 8, lambda k: h[:, k, c0:c0 + n], rh, n)
                    ti = t - 2 * hf
                    emit(DVE, lambda b=b, cv=cv, l0=l0: DVE.eng.tensor_tensor(
                        out=za_ap(j, l0, 512), in0=banks[b][:, 0:512], in1=cv[:, :], op=ALU.mult),
                        reads=[r_bank[b], cvres], writes=[r_za[j][ti]])
                if hf == 0:
                    emit(DVE, lambda: DVE.eng.tensor_copy(out=zcar[:, j, :], in_=zb_ap(zs, 1024, 2)),
                         reads=[zres], writes=[r_zcar[j]])
            for uu in range(2):
                uid = plan["u"][uu]
                s = load_unit(uid)
                for hl in range(4):
                    hd = 4 * uu + hl
                    ut = {}
                    for t in mt:
                        c0, n = TILES[t]
                        b = proj_job(s, uid, hl, 8, lambda k: h[:, k, c0:c0 + n], [r_h[k][t] for k in range(DC)], n)
                        u_ap, u_res = ringA.next()
                        emit(ACT, lambda b=b, u_ap=u_ap: ACT.eng.activation(
                            out=u_ap[:, :], in_=banks[b][:, 0:512], func=AF.Gelu_apprx_tanh),
                            reads=[r_bank[b]], writes=[u_res])
                        ut[t] = (u_ap, u_res)
                    for ti, t in enumerate(mt):
                        mms = []
                        for c in range(4):
                            i = ti * 4 + c
                            mms.append((lambda b, c=c: banks[b][:, c * 128:(c + 1) * 128],
                                        vn_ap(i, hd * 128, 128), wspT[:, hd, :], True, True))
                        b = pe_job(mms, [r_vn[ti * 4 + c] for c in range(4)] + [r_wspT])
                        sp_ap, sp_res = ringD.next()
                        for c in range(4):
                            emit(DVE, lambda b=b, c=c, sp_ap=sp_ap: DVE.eng.scalar_tensor_tensor(
                                out=sp_ap[:, c * 128:(c + 1) * 128], in0=banks[b][:, c * 128:(c + 1) * 128],
                                scalar=vecs[:, V_LNG + hd:V_LNG + hd + 1], in1=biast[:, hd, :],
                                op0=ALU.mult, op1=ALU.add),
                                reads=[r_bank[b], r_const, r_biast], writes=[sp_res])
                        u_ap, u_res = ut[t]
                        emit(DVE, lambda sp_ap=sp_ap, u_ap=u_ap, t=t: DVE.eng.tensor_tensor(
                            out=gb_ap(hd, lc(t), 512), in0=sp_ap[:, :], in1=u_ap[:, :], op=ALU.mult),
                            reads=[sp_res, u_res], writes=[r_gb[hd][ti]])
            for m in range(8):
                uid = plan["p3"][m]
                s = load_unit(uid)
                for ti, t in enumerate(mt):
                    c0, n = TILES[t]
                    l0 = lc(t)
                    rh = [r_h[k][t] for k in range(DC)]
                    b = proj_job(s, uid, 0, 8, lambda k: h[:, k, c0:c0 + n], rh, n)
                    sa, sares = ringA.next()
                    emit(ACT, lambda b=b, sa=sa: ACT.eng.activation(out=sa[:, :], in_=banks[b][:, 0:512],
                                                                     func=AF.Sigmoid),
                         reads=[r_bank[b]], writes=[sares])
                    b = proj_job(s, uid, 1, 8, lambda k: h[:, k, c0:c0 + n], rh, n)
                    sb, sbres = ringA.next()
                    emit(ACT, lambda b=b, sb=sb: ACT.eng.activation(out=sb[:, :], in_=banks[b][:, 0:512],
                                                                     func=AF.Sigmoid),
                         reads=[r_bank[b]], writes=[sbres])
                    b = proj_job(s, uid, 2, 8, lambda k: za_ap(k, l0, 512), [r_za[k][ti] for k in range(8)], n)
                    t1, t1res = ringD.next()
                    emit(DVE, lambda b=b, sa=sa, t1=t1: DVE.eng.tensor_tensor(
                        out=t1[:, :], in0=banks[b][:, 0:512], in1=sa[:, :], op=ALU.mult),
                        reads=[r_bank[b], sares], writes=[t1res])
                    b = proj_job(s, uid, 3, 8, lambda k: gb_ap(k, l0, 512), [r_gb[k][ti] for k in range(8)], n)
                    t2, t2res = ringD.next()
                    emit(DVE, lambda b=b, sb=sb, t2=t2: DVE.eng.tensor_tensor(
                        out=t2[:, :], in0=banks[b][:, 0:512], in1=sb[:, :], op=ALU.mult),
                        reads=[r_bank[b], sbres], writes=[t2res])
                    emit(DVE, lambda t1=t1, t2=t2, l0=l0: DVE.eng.tensor_tensor(
                        out=mg_ap(m, l0, 512), in0=t1[:, :], in1=t2[:, :], op=ALU.add),
                        reads=[t1res, t2res], writes=[r_mg[m][ti]] + r_vn)
            se = [load_unit(plan["mo"][q]) for q in range(2)]
            for ti, t in enumerate(mt):
                c0, n = TILES[t]
                l0 = lc(t)
                for q in range(2):
                    uid = plan["mo"][q]
                    s = se[q]
                    for mi in range(4):
                        m = 4 * q + mi
                        b = proj_job(s, uid, mi, 8, lambda k: mg_ap(k, l0, 512), [r_mg[k][ti] for k in range(8)], n)
                        emit(DVE, lambda b=b, m=m: DVE.eng.scalar_tensor_tensor(
                            out=x[:, m, c0:c0 + n], in0=banks[b][:, 0:n], scalar=v_gtw(1, m),
                            in1=x[:, m, c0:c0 + n], op0=ALU.mult, op1=ALU.add),
                            reads=[r_bank[b], r_x[m][t], r_adaG[1]], writes=[r_x[m][t]])
                        sq_emit(m, t)
                nxt.tile_done(t)

        def phase_fence(res_lists):
            pass

        ALL5 = [0, 1, 2, 3, 4]
        MAIN = [0, 1, 2, 3]

        def alias_fence():
            allres = ([r for row in r_sq for r in row] + [r for row in r_g for r in row] + r_vn +
                      [r for row in r_gb for r in row] + [r for row in r_za for r in row] +
                      [r for row in r_mg for r in row])
            toks = []
            wtoks = []
            for r in allres:
                toks.extend(r.r)
                if r.w is not None:
                    wtoks.append(r.w)
            best = {}
            for (c, v) in toks + wtoks:
                best[c] = max(best.get(c, 0), v)
            comp = [(c, v) for c, v in best.items()]
            for r in allres:
                r.r = list(comp)

        def spatial_setup():
            emit(POOL, lambda: POOL.eng.affine_select(
                out=wsp_f.rearrange("p (a b) -> p a b", a=8), in_=wsp_f.rearrange("p (a b) -> p a b", a=8),
                pattern=[[0, 8], [1, 128]], compare_op=ALU.is_ge, fill=0.0, base=0, channel_multiplier=-1),
                reads=[r_wsp], writes=[r_wsp])
            emit(DVE, lambda: DVE.eng.tensor_copy(out=wspT[:].rearrange("p a b -> p (a b)"), in_=wsp_f),
                 reads=[r_wsp], writes=[r_wspT])
            for q in range(2):
                mms = []
                for i in range(4):
                    hd = 4 * q + i
                    mms.append((lambda b, i=i: banks[b][:, i * 128:(i + 1) * 128], ones_f[:, :],
                                wsp_f[:, hd * 128:(hd + 1) * 128], True, True))
                b = pe_job(mms, [r_ones, r_wsp])
                for i in range(4):
                    hd = 4 * q + i
                    emit(DVE, lambda b=b, i=i, hd=hd: DVE.eng.scalar_tensor_tensor(
                        out=biast[:, hd, :], in0=banks[b][:, i * 128:(i + 1) * 128],
                        scalar=vecs[:, V_LNB + hd:V_LNB + hd + 1], in1=bsp_b[:, hd * 128:(hd + 1) * 128],
                        op0=ALU.mult, op1=ALU.add),
                        reads=[r_bank[b], r_const, r_bspb], writes=[r_biast])
            r_regF_phase.r = list(r_wsp.r) + list(r_bspb.r)
            r_regF_phase.w = r_wsp.w

        n1 = NormPipe(0)
        load_x(0)
        for j in range(DC):
            sq_emit(j, 0)
        n1._ab(0, 6)
        ada_units(0, [0, 1])
        load_x(1, after=ada_tok[(0, 1)])
        for j in range(DC):
            sq_emit(j, 1)
        n1._ab(1, 7)
        spatial_setup()
        ada_units(0, [2])
        load_x(2, after=ada_tok[(0, 2)])
        ada_units(0, [3])
        load_x(3, after=ada_tok[(0, 3)])
        load_x(4)
        pro_rr = {}
        for t, (o, w) in zip((2, 3, 4), ((0, 512), (512, 512), (1024, 2))):
            for j in range(DC):
                sq_emit(j, t)
            dres = Res()
            dst = regF[:, o:o + 512] if w == 512 else regF[:, o:o + 2]
            n1._ab_sb(t, dst, dres, [r_regF_phase])
            pro_rr[t] = (dst, dres)
        n1._c(0, banks[6], r_bank[6])
        n1._c(1, banks[7], r_bank[7])

        def pro_pre_tile(t):
            if t + 2 <= 4:
                dst, dres = pro_rr[t + 2]
                n1._c(t + 2, dst, dres, extra_r=[r_regF_phase])

        ada_q = [(0, 4), (0, 5)] + [(1, q) for q in range(6)] + [(2, q) for q in range(6)]

        def pop_ada():
            if ada_q:
                p, q = ada_q.pop(0)
                ada_units(p, [q])

        n2 = NormPipe(1)
        ffn("ffn1", 0, ALL5, n2, pre_tile=pro_pre_tile, post_unit=pop_ada)
        assert not ada_q
        for s_ in range(2):
            r_zb[s_].r.extend(r_regF_phase.r)
            if r_regF_phase.w is not None:
                r_zb[s_].r.append(r_regF_phase.w)
        alias_fence()
        n3 = NormPipe(2)
        mix_half(0, n3)
        alias_fence()
        mix_half(1, n3)
        n3.flush()
        alias_fence()
        nf = NormPipe(None, final=True)
        ffn("ffn2", 2, MAIN, nf)
        streams["sp"].append({"m": "wait_ge", "a": (c_st.sem, c_st.val), "kw": {}, "inc": None})

        block = E(nc.Block())

        def replay(eng, items):
            for it in items:
                ins = getattr(eng, it["m"])(*it["a"], **it["kw"])
                if it["inc"] is not None:
                    ins.then_inc(*it["inc"])

        @block.tensor
        def _(e):
            replay(e, streams["pe"])

        @block.scalar
        def _(e):
            replay(e, streams["act"])

        @block.vector
        def _(e):
            replay(e, streams["dve"])

        @block.gpsimd
        def _(e):
            replay(e, streams["pool"])

        @block.sync
        def _(e):
            replay(e, streams["sp"])
    return nc


def kernel(x, c, w_ada, b_ada, g_ffn1, w_ffn1_in, w_ffn1_out, g_mix, w_mix_in, conv_w, ln_v_g, ln_v_b,
           w_spatial, b_spatial, w_a_out, w_b_out, w_mix_out, g_ffn2, w_ffn2_in, w_ffn2_out, g_final):
    f = lambda a: np.asarray(a, dtype=np.float32)
    x = f(x)
    W = {k: f(v) for k, v in dict(w_ada=w_ada, w_ffn1_in=w_ffn1_in, w_ffn1_out=w_ffn1_out, w_mix_in=w_mix_in,
                                  w_a_out=w_a_out, w_b_out=w_b_out, w_mix_out=w_mix_out, w_ffn2_in=w_ffn2_in,
                                  w_ffn2_out=w_ffn2_out).items()}
    stream, units, plan = build_weight_stream(W)
    fm = lambda v: f(v).reshape(8, 128).T
    wsp = np.ascontiguousarray(f(w_spatial).transpose(2, 0, 1))
    bsp = np.ascontiguousarray(np.broadcast_to(f(b_spatial).reshape(1, 1024), (128, 1024)))
    cw = f(conv_w)
    in_maps = []
    per_b = NCORES // x.shape[0]
    for core in range(NCORES):
        b = core // per_b
        s0 = (core % per_b) * TOK
        xs = x[b, s0:s0 + TOK, :]
        xt = np.zeros((128, DC, TH), np.float32)
        xt[:, :, :TOK] = xs.reshape(TOK, DC, 128).transpose(2, 1, 0)
        hmv = 0.0
        if s0 > 0:
            xt[:, :, TOK:TH] = x[b, s0 - 2:s0, :].reshape(2, DC, 128).transpose(2, 1, 0)
            hmv = 1.0
        vecs = np.zeros((128, NV), np.float32)
        vecs[:, V_G1:V_G1 + 8] = fm(g_ffn1)
        vecs[:, V_GM:V_GM + 8] = fm(g_mix)
        vecs[:, V_G2:V_G2 + 8] = fm(g_ffn2)
        vecs[:, V_GF:V_GF + 8] = fm(g_final)
        vecs[:, V_LNG:V_LNG + 8] = fm(ln_v_g)
        vecs[:, V_LNB:V_LNB + 8] = fm(ln_v_b)
        for k in range(3):
            vecs[:, V_CW + 8 * k:V_CW + 8 * k + 8] = fm(cw[k])
        vecs[:, V_BADA:V_BADA + 72] = f(b_ada).reshape(72, 128).T
        vecs[:, V_C:V_C + 8] = fm(f(c)[b])
        in_maps.append({"xT": xt, "vecs": vecs, "wsp": wsp, "bsp": bsp,
                        "hmask": np.full((128, 1), hmv, np.float32), "wst": stream})
    nc = build_program(units, plan, stream.shape[0])
    res = run_bass_kernel_spmd(nc, in_maps, core_ids=list(range(NCORES)))
    out = np.empty((x.shape[0], SEQ, D), np.float32)
    for core in range(NCORES):
        b = core // per_b
        s0 = (core % per_b) * TOK
        yt = np.asarray(res.results[core]["yT"])
        out[b, s0:s0 + TOK, :] = yt.transpose(2, 1, 0).reshape(TOK, D)
    return out
```

```python
import numpy as np
from contextlib import ExitStack
import concourse.bass as bass
import concourse.mybir as mybir
from concourse.bass_utils import run_bass_kernel_spmd

F32 = mybir.dt.float32
BF16 = mybir.dt.bfloat16
AF = mybir.ActivationFunctionType
ALU = mybir.AluOpType

D = 1024
DC = 8
TOK = 2048
TH = 2050
DFF = 2816
FC = 22
FH = 11
EPS = 1e-6
NCORES = 8
SEQ = 8192
TILES = [(0, 512), (512, 512), (1024, 512), (1536, 512), (2048, 2)]
HALO = 4
NSLOT = 4

V_G1, V_GM, V_G2, V_GF, V_LNG, V_LNB, V_CW, V_BADA, V_C = 0, 8, 16, 24, 32, 40, 48, 72, 144
NV = 152


class UnitMeta:
    def __init__(self, off, kc, w):
        self.off, self.kc, self.w = off, kc, w
        self.L = kc * w


def build_weight_stream(W):
    units = []
    bufs = []
    pos = [0]

    def add_unit(mats):
        kc = len(mats[0][1])
        blocks = []
        for (M, rcs, c0) in mats:
            rows = np.concatenate([np.arange(r * 128, r * 128 + 128) for r in rcs])
            blk = M[rows, c0:c0 + 128].reshape(kc, 128, 128).transpose(1, 0, 2)
            blocks.append(blk)
        u = np.concatenate(blocks, axis=2)
        w = u.shape[2]
        units.append(UnitMeta(pos[0], kc, w))
        bufs.append(np.ascontiguousarray(u, dtype=np.float32).reshape(-1))
        pos[0] += 128 * kc * w
        return len(units) - 1

    K8 = list(range(8))
    plan = {}
    plan["ada"] = []
    for part in range(3):
        us = []
        for q in range(6):
            c0 = part * 3072 + q * 512
            us.append(add_unit([(W["w_ada"], K8, c0 + 128 * i) for i in range(4)]))
        plan["ada"].append(us)
    for name, win, wout in (("ffn1", "w_ffn1_in", "w_ffn1_out"), ("ffn2", "w_ffn2_in", "w_ffn2_out")):
        halves = []
        for hh in range(2):
            seq = []
            for j in range(hh * FH, hh * FH + FH):
                seq.append(("a", j, (W[win], K8, j * 128)))
                seq.append(("b", j, (W[win], K8, DFF + j * 128)))
            in_units = []
            for i in range(0, len(seq), 4):
                grp = seq[i:i + 4]
                uid = add_unit([g[2] for g in grp])
                in_units.append((uid, [(g[0], g[1]) for g in grp]))
            out_units = []
            rcs = list(range(hh * FH, hh * FH + FH))
            for m0 in range(0, 8, 2):
                uid = add_unit([(W[wout], rcs, m * 128) for m in (m0, m0 + 1)])
                out_units.append((uid, [m0, m0 + 1]))
            halves.append((in_units, out_units))
        plan[name] = halves
    wm = W["w_mix_in"]
    O_GA, O_GB, O_B, O_C, O_X, O_U, O_V = [i * 1024 for i in range(7)]
    plan["v"] = [add_unit([(wm, K8, O_V + uh * 512 + 128 * i) for i in range(4)]) for uh in range(2)]
    plan["u"] = [add_unit([(wm, K8, O_U + uu * 512 + 128 * i) for i in range(4)]) for uu in range(2)]
    plan["cxb"] = [add_unit([(wm, K8, O_C + j * 128), (wm, K8, O_X + j * 128), (wm, K8, O_B + j * 128)])
                   for j in range(8)]
    plan["p3"] = [add_unit([(wm, K8, O_GA + m * 128), (wm, K8, O_GB + m * 128),
                            (W["w_a_out"], K8, m * 128), (W["w_b_out"], K8, m * 128)]) for m in range(8)]
    plan["mo"] = [add_unit([(W["w_mix_out"], K8, (4 * q + i) * 128) for i in range(4)]) for q in range(2)]
    stream = np.concatenate(bufs)
    return stream, units, plan


class Ctr:
    def __init__(self, sem, step):
        self.sem, self.step, self.val = sem, step, 0


class Res:
    __slots__ = ("w", "r")

    def __init__(self):
        self.w = None
        self.r = []


class Q:
    def __init__(self, eng, ctr=None):
        self.eng, self.ctr, self.known = eng, ctr, {}

    def wait(self, ctr, val):
        if self.known.get(ctr, 0) >= val:
            return
        self.eng.wait_ge(ctr.sem, val)
        self.known[ctr] = val


def _deps(q, reads, writes):
    deps = {}
    for r in reads:
        if r.w is not None:
            c, v = r.w
            deps[c] = max(deps.get(c, 0), v)
    for w in writes:
        if w.w is not None:
            c, v = w.w
            deps[c] = max(deps.get(c, 0), v)
        for (c, v) in w.r:
            deps[c] = max(deps.get(c, 0), v)
    for c, v in deps.items():
        q.wait(c, v)


def emit(q, fn, reads=(), writes=(), ctr=None):
    _deps(q, reads, writes)
    ins = fn()
    c = ctr if ctr is not None else q.ctr
    c.val += c.step
    ins.then_inc(c.sem, c.step)
    tok = (c, c.val)
    for r in reads:
        r.r.append(tok)
    for w in writes:
        w.w = tok
        w.r = []
    return tok


class Ring:
    def __init__(self, aps):
        self.aps = aps
        self.res = [Res() for _ in aps]
        self.i = 0

    def next(self):
        k = self.i % len(self.aps)
        self.i += 1
        return self.aps[k], self.res[k]


def build_program(units, plan, total_w):
    nc = bass.Bass("TRN2", target_bir_lowering=False)
    xT_d = nc.dram_tensor("xT", [128, DC, TH], F32, kind="ExternalInput").ap()
    vecs_d = nc.dram_tensor("vecs", [128, NV], F32, kind="ExternalInput").ap()
    wsp_d = nc.dram_tensor("wsp", [128, 8, 128], F32, kind="ExternalInput").ap()
    bsp_d = nc.dram_tensor("bsp", [128, 1024], F32, kind="ExternalInput").ap()
    hm_d = nc.dram_tensor("hmask", [128, 1], F32, kind="ExternalInput").ap()
    wst_d = nc.dram_tensor("wst", [total_w], F32, kind="ExternalInput").ap()
    yT_d = nc.dram_tensor("yT", [128, DC, TOK], F32, kind="ExternalOutput").ap()

    with ExitStack() as es:
        E = es.enter_context
        x = E(nc.sbuf_tensor("x", [128, DC, TH], F32))
        h = E(nc.sbuf_tensor("h", [128, DC, TH], BF16))
        regB = E(nc.sbuf_tensor("regB", [128, 24576], BF16))
        regF = E(nc.sbuf_tensor("regF", [128, 2 * 1026], F32))
        wslots = [E(nc.sbuf_tensor(f"wslot{i}", [128, 4096], BF16)) for i in range(NSLOT)]
        tA = [E(nc.sbuf_tensor(f"tA{i}", [128, 512], F32)) for i in range(4)]
        tD = [E(nc.sbuf_tensor(f"tD{i}", [128, 512], F32)) for i in range(3)]
        vecs = E(nc.sbuf_tensor("vecs_sb", [128, NV], F32))
        adaT = E(nc.sbuf_tensor("adaT", [128, 72], F32))
        der = E(nc.sbuf_tensor("der", [128, 48], F32))
        small = E(nc.sbuf_tensor("small", [128, 64], F32))
        hm = E(nc.sbuf_tensor("hm", [128, 1], F32))
        sc_bf = E(nc.sbuf_tensor("sc_bf", [128, 8, 2], BF16))
        ones_bf = E(nc.sbuf_tensor("ones_bf", [128, 128], BF16))
        ones_f = E(nc.sbuf_tensor("ones_f", [128, 128], F32))
        wspT = E(nc.sbuf_tensor("wspT", [128, 8, 128], BF16))
        biast = E(nc.sbuf_tensor("biast", [128, 8, 128], F32))
        zcar = E(nc.sbuf_tensor("zcar", [128, 8, 2], F32))
        stats = E(nc.sbuf_tensor("stats", [128, 2, 2, 6], F32))
        banks = [E(nc.psum_tensor(f"ps{i}", [128, 512], F32)) for i in range(8)]

        s_pe = E(nc.semaphore("s_pe"))
        s_act = E(nc.semaphore("s_act"))
        s_dve = E(nc.semaphore("s_dve"))
        s_pool = E(nc.semaphore("s_pool"))
        s_ld = E(nc.semaphore("s_ld"))
        s_x = [E(nc.semaphore(f"s_x{i}")) for i in range(DC)]
        s_st = E(nc.semaphore("s_st"))
        s_w = [E(nc.semaphore(f"s_w{i}")) for i in range(NSLOT)]

        c_pe, c_act, c_dve, c_pool = Ctr(s_pe, 1), Ctr(s_act, 1), Ctr(s_dve, 1), Ctr(s_pool, 1)
        c_ld, c_st = Ctr(s_ld, 16), Ctr(s_st, 16)
        c_x = [Ctr(s, 16) for s in s_x]
        c_w = [Ctr(s, 16) for s in s_w]

        streams = {"pe": [], "act": [], "dve": [], "pool": [], "sp": []}

        class Rec:
            def __init__(self, name):
                self.name = name

            def __getattr__(self, meth):
                def call(*a, **kw):
                    item = {"m": meth, "a": a, "kw": kw, "inc": None}
                    streams[self.name].append(item)

                    class H:
                        def then_inc(_s, sem, n):
                            item["inc"] = (sem, n)
                            return _s
                    return H()
                return call

        PE, ACT, DVE, POOL, SP = (Q(Rec("pe"), c_pe), Q(Rec("act"), c_act), Q(Rec("dve"), c_dve),
                                  Q(Rec("pool"), c_pool), Q(Rec("sp"), None))

        r_x = [[Res() for _ in TILES] for _ in range(DC)]
        r_h = [[Res() for _ in TILES] for _ in range(DC)]
        r_sq = [[Res() for _ in TILES] for _ in range(DC)]
        r_g = [[Res() for _ in TILES] for _ in range(FH)]
        r_bank = [Res() for _ in range(8)]
        r_wslot = [Res() for _ in range(NSLOT)]
        r_const = Res()
        r_sc = Res()
        r_ada = [Res() for _ in range(3)]
        r_der = [Res() for _ in range(3)]
        r_wsp = Res()
        r_bspb = Res()
        r_wspT = Res()
        r_biast = Res()
        r_ones = Res()
        r_small = Res()
        r_stats = [Res(), Res()]
        r_ln = [Res(), Res()]
        r_vn = [Res() for _ in range(8)]
        r_gb = [[Res(), Res()] for _ in range(8)]
        r_za = [[Res(), Res()] for _ in range(8)]
        r_mg = [[Res(), Res()] for _ in range(8)]
        r_zb = [Res(), Res()]
        r_zcar = [Res() for _ in range(8)]
        ringA = Ring([t[:] for t in tA])
        ringD = Ring([t[:] for t in tD])

        def g_ap(jj, c0, n):
            return regB[:, jj * TH + c0: jj * TH + c0 + n]

        def sq_ap(j, c0, n):
            return regB[:, j * TH + c0: j * TH + c0 + n]

        def vn_ap(i, c0, n):
            return regB[:, i * 1024 + c0: i * 1024 + c0 + n]

        def mg_ap(m, c0, n):
            return regB[:, m * 1024 + c0: m * 1024 + c0 + n]

        def gb_ap(hd, c0, n):
            return regB[:, 8192 + hd * 1024 + c0: 8192 + hd * 1024 + c0 + n]

        def za_ap(j, c0, n):
            return regB[:, 16384 + j * 1024 + c0: 16384 + j * 1024 + c0 + n]

        def zb_ap(s, c0, n):
            return regF[:, s * 1026 + c0: s * 1026 + c0 + n]

        wsp_f = regF[:, 0:1024]
        bsp_b = regF[:, 1024:2048]

        r_regB_phase = Res()
        r_regF_phase = Res()

        job_ctr = [0]
        wstate = {"next_slot": 0, "unit_slot": {}, "uses": 0}

        def load_unit(uid):
            s = wstate["next_slot"] % NSLOT
            wstate["next_slot"] += 1
            um = units[uid]
            src = wst_d[um.off: um.off + 128 * um.L].rearrange("(p k w) -> p k w", p=128, k=um.kc)
            dst = wslots[s][:, 0:um.L].rearrange("p (k w) -> p k w", k=um.kc)
            wstate["last_tok"] = emit(POOL, lambda: POOL.eng.dma_start(out=dst, in_=src), reads=[],
                                      writes=[r_wslot[s]], ctr=c_w[s])
            return s

        def wchunk(s, uid, k, mi):
            um = units[uid]
            o = k * um.w + mi * 128
            return wslots[s][:, o:o + 128]

        def pe_job(mms, reads, bank=None):
            if bank is None:
                b = job_ctr[0] % 6
                job_ctr[0] += 1
            else:
                b = bank
            _deps(PE, reads, [r_bank[b]])
            last = None
            for (o, l, r, st, sp) in mms:
                last = PE.eng.matmul(o(b), l, r, start=st, stop=sp)
            c_pe.val += 1
            last.then_inc(c_pe.sem, 1)
            tok = (c_pe, c_pe.val)
            for r in reads:
                r.r.append(tok)
            r_bank[b].w = tok
            r_bank[b].r = []
            return b

        def proj_job(s, uid, mi, kc, rhs_fn, rhs_res, n, extra_reads=()):
            mms = []
            for k in range(kc):
                mms.append((lambda b, n=n: banks[b][:, 0:n], wchunk(s, uid, k, mi), rhs_fn(k), k == 0, k == kc - 1))
            return pe_job(mms, [r_wslot[s]] + list(rhs_res) + list(extra_reads))

        emit(SP, lambda: SP.eng.dma_start(out=wsp_f.rearrange("p (a b) -> p a b", a=8), in_=wsp_d[:, :, :]),
             writes=[r_wsp], ctr=c_ld)
        emit(SP, lambda: SP.eng.dma_start(out=bsp_b, in_=bsp_d[:, :]), writes=[r_bspb], ctr=c_ld)
        emit(SP, lambda: SP.eng.dma_start(out=vecs[:], in_=vecs_d[:, :]), writes=[r_const], ctr=c_ld)
        emit(SP, lambda: SP.eng.dma_start(out=hm[:], in_=hm_d[:, :]), writes=[r_const], ctr=c_ld)
        tot = (c_ld, c_ld.val)
        for r in [r_const, r_wsp, r_bspb]:
            r.w = tot
        def load_x(t, after=None):
            c0, n = TILES[t]
            if after is not None:
                SP.wait(*after)
            emit(SP, lambda: SP.eng.dma_start(out=x[:, :, c0:c0 + n], in_=xT_d[:, :, c0:c0 + n]),
                 writes=[r_x[j][t] for j in range(DC)], ctr=c_x[t])

        emit(DVE, lambda: DVE.eng.memset(ones_bf[:], 1.0), writes=[r_ones])
        emit(DVE, lambda: DVE.eng.memset(ones_f[:], 1.0), writes=[r_ones])
        emit(DVE, lambda: DVE.eng.memset(small[:, 0:1], EPS), writes=[r_small])
        emit(DVE, lambda: DVE.eng.memset(small[:, 1:2], -0.5), writes=[r_small])
        for dup in range(2):
            emit(ACT, lambda dup=dup: ACT.eng.activation(out=sc_bf[:, :, dup], in_=vecs[:, V_C:V_C + 8], func=AF.Silu),
                 reads=[r_const], writes=[r_sc])
        G_COL = [V_G1, V_GM, V_G2]

        ada_tok = {}
        r_adaS = [Res() for _ in range(3)]
        r_adaG = [Res() for _ in range(3)]

        def ada_units(part, qs):
            for q in qs:
                uid = plan["ada"][part][q]
                s = load_unit(uid)
                ada_tok[(part, q)] = wstate["last_tok"]
                mms = []
                for i in range(4):
                    for k in range(8):
                        mms.append((lambda b, i=i: banks[b][:, 2 * i:2 * i + 2], wchunk(s, uid, k, i),
                                    sc_bf[:, k, :], k == 0, k == 7))
                b = pe_job(mms, [r_wslot[s], r_sc])
                c0 = part * 24 + q * 4
                rr = r_adaS[part] if q < 4 else r_adaG[part]
                emit(DVE, lambda b=b, c0=c0: DVE.eng.tensor_tensor(
                    out=adaT[:, c0:c0 + 4], in0=banks[b][:, 0:8].rearrange("p (a b) -> p a b", b=2)[:, :, 0],
                    in1=vecs[:, V_BADA + c0:V_BADA + c0 + 4], op=ALU.add),
                    reads=[r_bank[b], r_const], writes=[rr])
            a0 = part * 24
            if 3 in qs:
                emit(DVE, lambda: DVE.eng.scalar_tensor_tensor(
                    out=der[:, part * 16:part * 16 + 8], in0=adaT[:, a0 + 8:a0 + 16], scalar=1.0,
                    in1=vecs[:, G_COL[part]:G_COL[part] + 8], op0=ALU.add, op1=ALU.mult),
                    reads=[r_adaS[part], r_const], writes=[r_adaS[part]])
            if 5 in qs:
                emit(DVE, lambda: DVE.eng.tensor_scalar(
                    out=der[:, part * 16 + 8:part * 16 + 16], in0=adaT[:, a0 + 16:a0 + 24],
                    scalar1=(1.0 if part == 1 else 0.5), scalar2=None, op0=ALU.mult),
                    reads=[r_adaG[part]], writes=[r_adaG[part]])

        def v_scale(part, j):
            return der[:, part * 16 + j: part * 16 + j + 1]

        def v_gtw(part, j):
            return der[:, part * 16 + 8 + j: part * 16 + 8 + j + 1]

        def v_shift(part, j):
            return adaT[:, part * 24 + j: part * 24 + j + 1]

        def sq_emit(j, t):
            c0, n = TILES[t]
            emit(ACT, lambda: ACT.eng.activation(out=h[:, j, c0:c0 + n], in_=x[:, j, c0:c0 + n], func=AF.Square),
                 reads=[r_x[j][t]], writes=[r_h[j][t]])

        class NormPipe:
            def __init__(self, part, final=False):
                self.part, self.final, self.prev, self.cnt = part, final, None, 0

            def tile_done(self, t):
                if self.prev is not None:
                    self._emit(self.prev)
                self.prev = t

            def flush(self):
                if self.prev is not None:
                    self._emit(self.prev)
                    self.prev = None

            def _emit(self, t):
                nb = 6 + (self.cnt % 2)
                self.cnt += 1
                self._ab(t, nb)
                self._c(t, banks[nb], r_bank[nb])

            def _ss(self, t, bank):
                c0, n = TILES[t]
                mms = []
                for j in range(DC):
                    mms.append((lambda b: banks[b][:, 0:n], ones_bf[:, :], h[:, j, c0:c0 + n], j == 0, j == DC - 1))
                return pe_job(mms, [r_ones] + [r_h[j][t] for j in range(DC)], bank=bank)

            def _ab(self, t, nb):
                c0, n = TILES[t]
                self._ss(t, nb)
                pb = banks[nb]
                emit(ACT, lambda: ACT.eng.activation(out=pb[:, 0:n], in_=pb[:, 0:n], func=AF.Sqrt,
                                                      bias=small[:, 0:1], scale=1.0 / D),
                     reads=[r_bank[nb], r_small], writes=[r_bank[nb]])
                emit(DVE, lambda: DVE.eng.reciprocal(out=pb[:, 0:n], in_=pb[:, 0:n]),
                     reads=[r_bank[nb]], writes=[r_bank[nb]])

            def _ab_sb(self, t, dst, dres, extra_w):
                c0, n = TILES[t]
                b = self._ss(t, None)
                pb = banks[b]
                emit(ACT, lambda: ACT.eng.activation(out=pb[:, 0:n], in_=pb[:, 0:n], func=AF.Sqrt,
                                                      bias=small[:, 0:1], scale=1.0 / D),
                     reads=[r_bank[b], r_small], writes=[r_bank[b]])
                emit(DVE, lambda: DVE.eng.reciprocal(out=dst[:, 0:n], in_=pb[:, 0:n]),
                     reads=[r_bank[b]], writes=[dres] + list(extra_w))

            def _c(self, t, rr, rres, extra_r=()):
                part, final = self.part, self.final
                c0, n = TILES[t]
                for j in range(DC):
                    if final:
                        emit(DVE, lambda j=j: DVE.eng.scalar_tensor_tensor(
                            out=x[:, j, c0:c0 + n], in0=x[:, j, c0:c0 + n], scalar=vecs[:, V_GF + j:V_GF + j + 1],
                            in1=rr[:, 0:n], op0=ALU.mult, op1=ALU.mult),
                            reads=[r_x[j][t], rres, r_const], writes=[r_x[j][t]])
                        if j % 2 == 1:
                            emit(SP, lambda j=j: SP.eng.dma_start(out=yT_d[:, j - 1:j + 1, c0:c0 + n],
                                                                   in_=x[:, j - 1:j + 1, c0:c0 + n]),
                                 reads=[r_x[j - 1][t], r_x[j][t]], ctr=c_st)
                    else:
                        ht, htres = ringD.next()
                        emit(DVE, lambda j=j, ht=ht: DVE.eng.scalar_tensor_tensor(
                            out=ht[:, 0:n], in0=x[:, j, c0:c0 + n], scalar=v_scale(part, j),
                            in1=rr[:, 0:n], op0=ALU.mult, op1=ALU.mult),
                            reads=[r_x[j][t], rres, r_adaS[part]] + list(extra_r), writes=[htres])
                        emit(ACT, lambda j=j, ht=ht: ACT.eng.activation(
                            out=h[:, j, c0:c0 + n], in_=ht[:, 0:n], func=AF.Identity,
                            bias=v_shift(part, j), scale=1.0),
                            reads=[htres, r_adaS[part]], writes=[r_h[j][t]])

        def ffn(name, part, tiles, nxt, pre_tile=None, post_unit=None):
            first = True
            for hh in range(2):
                in_units, out_units = plan[name][hh]
                for (uid, chunks) in in_units:
                    s = load_unit(uid)
                    for t in tiles:
                        c0, n = TILES[t]
                        st_ap = None
                        for mi, (kind, j) in enumerate(chunks):
                            b = proj_job(s, uid, mi, 8, lambda k: h[:, k, c0:c0 + n],
                                         [r_h[k][t] for k in range(DC)], n)
                            if kind == "a":
                                st_ap, st_res = ringA.next()
                                emit(ACT, lambda b=b, st_ap=st_ap: ACT.eng.activation(
                                    out=st_ap[:, 0:n], in_=banks[b][:, 0:n], func=AF.Silu),
                                    reads=[r_bank[b]], writes=[st_res])
                            else:
                                jj = j - hh * FH
                                emit(DVE, lambda b=b, st_ap=st_ap, jj=jj: DVE.eng.tensor_tensor(
                                    out=g_ap(jj, c0, n), in0=banks[b][:, 0:n], in1=st_ap[:, 0:n], op=ALU.mult),
                                    reads=[r_bank[b], st_res], writes=[r_g[jj][t]])
                        if first and pre_tile is not None:
                            pre_tile(t)
                    first = False
                    if post_unit is not None:
                        post_unit()

                def out_jobs(s, uid, ms, t):
                    c0, n = TILES[t]
                    for mi, m in enumerate(ms):
                        b = proj_job(s, uid, mi, FH, lambda k: g_ap(k, c0, n),
                                     [r_g[k][t] for k in range(FH)], n)
                        emit(DVE, lambda b=b, m=m: DVE.eng.scalar_tensor_tensor(
                            out=x[:, m, c0:c0 + n], in0=banks[b][:, 0:n], scalar=v_gtw(part, m),
                            in1=x[:, m, c0:c0 + n], op0=ALU.mult, op1=ALU.add),
                            reads=[r_bank[b], r_x[m][t], r_adaG[part]], writes=[r_x[m][t]])
                        if hh == 1:
                            sq_emit(m, t)

                if hh == 0:
                    for (uid, ms) in out_units:
                        s = load_unit(uid)
                        for t in tiles:
                            out_jobs(s, uid, ms, t)
                        if post_unit is not None:
                            post_unit()
                else:
                    for p0 in (0, 2):
                        pair = out_units[p0:p0 + 2]
                        ss = [load_unit(uid) for (uid, ms) in pair]
                        for t in tiles:
                            for s, (uid, ms) in zip(ss, pair):
                                out_jobs(s, uid, ms, t)
                            if p0 == 2:
                                nxt.tile_done(t)
            nxt.flush()

        def mix_half(hf, nxt):
            mt = [2 * hf, 2 * hf + 1]
            base = 1024 * hf

            def lc(t):
                return TILES[t][0] - base

            sv = [load_unit(plan["v"][0]), load_unit(plan["v"][1])]
            pre_c = {0: load_unit(plan["cxb"][0]), 1: load_unit(plan["cxb"][1])}

            def ln_tail(i, halves):
                p = i % 2
                sb = 8 + 4 * p
                emit(DVE, lambda: DVE.eng.bn_aggr(out=small[:, sb:sb + 2],
                                                   in_=stats[:, p, :, :].rearrange("p a b -> p (a b)")),
                     reads=[r_stats[p]], writes=[r_ln[p]])
                emit(DVE, lambda: DVE.eng.tensor_scalar(out=small[:, sb + 2:sb + 3], in0=small[:, sb + 1:sb + 2],
                                                         scalar1=EPS, scalar2=None, op0=ALU.add),
                     reads=[r_ln[p]], writes=[r_ln[p]])
                emit(POOL, lambda: POOL.eng.tensor_tensor(out=small[:, sb + 3:sb + 4], in0=small[:, sb + 2:sb + 3],
                                                           in1=small[:, 1:2], op=ALU.pow),
                     reads=[r_ln[p], r_small], writes=[r_ln[p]])
                for uh in range(2):
                    gv, gvres = halves[uh]
                    emit(DVE, lambda gv=gv, uh=uh: DVE.eng.tensor_scalar(
                        out=vn_ap(i, uh * 512, 512), in0=gv[:, :], scalar1=small[:, sb:sb + 1],
                        scalar2=small[:, sb + 3:sb + 4], op0=ALU.subtract, op1=ALU.mult),
                        reads=[gvres, r_ln[p]], writes=[r_vn[i]])

            pend = None
            for i in range(8):
                tcol = base + 128 * i
                tt = tcol // 512
                halves = []
                for uh in range(2):
                    s = sv[uh]
                    mms = []
                    for k in range(8):
                        mms.append((lambda b: banks[b][:, 0:512], h[:, k, tcol:tcol + 128],
                                    wslots[s][:, k * 512:(k + 1) * 512], k == 0, k == 7))
                    b = pe_job(mms, [r_wslot[s]] + [r_h[k][tt] for k in range(DC)])
                    gv, gvres = ringA.next()
                    emit(ACT, lambda b=b, gv=gv: ACT.eng.activation(out=gv[:, :], in_=banks[b][:, 0:512],
                                                                     func=AF.Gelu_apprx_tanh),
                         reads=[r_bank[b]], writes=[gvres])
                    emit(DVE, lambda gv=gv, uh=uh, i=i: DVE.eng.bn_stats(out=stats[:, i % 2, uh, :], in_=gv[:, :]),
                         reads=[gvres], writes=[r_stats[i % 2]])
                    halves.append((gv, gvres))
                if pend is not None:
                    ln_tail(*pend)
                pend = (i, halves)
            ln_tail(*pend)
            for j in range(8):
                uid = plan["cxb"][j]
                s = pre_c[j] if j in pre_c else load_unit(uid)
                zs = j % 2
                zres = r_zb[zs]
                tl = ([HALO] if hf == 0 else []) + mt
                if hf == 1:
                    emit(DVE, lambda: DVE.eng.tensor_copy(out=zb_ap(zs, 0, 2), in_=zcar[:, j, :]),
                         reads=[r_zcar[j]], writes=[zres])
                for t in tl:
                    c0, n = TILES[t]
                    rh = [r_h[k][t] for k in range(DC)]
                    b = proj_job(s, uid, 0, 8, lambda k: h[:, k, c0:c0 + n], rh, n)
                    ct, ctres = ringA.next()
                    emit(ACT, lambda b=b, ct=ct: ACT.eng.activation(out=ct[:, 0:n], in_=banks[b][:, 0:n],
                                                                     func=AF.Copy),
                         reads=[r_bank[b]], writes=[ctres])
                    b = proj_job(s, uid, 1, 8, lambda k: h[:, k, c0:c0 + n], rh, n)
                    if t == HALO:
                        emit(DVE, lambda b=b, ct=ct: DVE.eng.scalar_tensor_tensor(
                            out=zb_ap(zs, 0, 2), in0=banks[b][:, 0:2], scalar=hm[:, 0:1], in1=ct[:, 0:2],
                            op0=ALU.mult, op1=ALU.mult),
                            reads=[r_bank[b], ctres, r_const], writes=[zres])
                        continue
                    l0 = lc(t)
                    emit(DVE, lambda b=b, ct=ct, l0=l0: DVE.eng.tensor_tensor(
                        out=zb_ap(zs, 2 + l0, 512), in0=banks[b][:, 0:512], in1=ct[:, :], op=ALU.mult),
                        reads=[r_bank[b], ctres], writes=[zres])
                    cv, cvres = ringD.next()
                    emit(ACT, lambda cv=cv, l0=l0: ACT.eng.activation(
                        out=cv[:, :], in_=zb_ap(zs, l0, 512), func=AF.Identity,
                        scale=vecs[:, V_CW + j:V_CW + j + 1]),
                        reads=[zres, r_const], writes=[cvres])
                    for kk in (1, 2):
                        emit(DVE, lambda cv=cv, l0=l0, kk=kk: DVE.eng.scalar_tensor_tensor(
                            out=cv[:, :], in0=zb_ap(zs, l0 + kk, 512),
                            scalar=vecs[:, V_CW + 8 * kk + j:V_CW + 8 * kk + j + 1], in1=cv[:, :],
                            op0=ALU.mult, op1=ALU.add),
                            reads=[zres, cvres, r_const], writes=[cvres])
                    b = proj_job(s, uid, 2, 8, lambda k: h[:, k, c0:c0 + n], rh, n)
                    ti = t - 2 * hf
                    emit(DVE, lambda b=b, cv=cv, l0=l0: DVE.eng.tensor_tensor(
                        out=za_ap(j, l0, 512), in0=banks[b][:, 0:512], in1=cv[:, :], op=ALU.mult),
                        reads=[r_bank[b], cvres], writes=[r_za[j][ti]])
                if hf == 0:
                    emit(DVE, lambda: DVE.eng.tensor_copy(out=zcar[:, j, :], in_=zb_ap(zs, 1024, 2)),
                         reads=[zres], writes=[r_zcar[j]])
            for uu in range(2):
                uid = plan["u"][uu]
                s = load_unit(uid)
                for hl in range(4):
                    hd = 4 * uu + hl
                    ut = {}
                    for t in mt:
                        c0, n = TILES[t]
                        b = proj_job(s, uid, hl, 8, lambda k: h[:, k, c0:c0 + n], [r_h[k][t] for k in range(DC)], n)
                        u_ap, u_res = ringA.next()
                        emit(ACT, lambda b=b, u_ap=u_ap: ACT.eng.activation(
                            out=u_ap[:, :], in_=banks[b][:, 0:512], func=AF.Gelu_apprx_tanh),
                            reads=[r_bank[b]], writes=[u_res])
                        ut[t] = (u_ap, u_res)
                    for ti, t in enumerate(mt):
                        mms = []
                        for c in range(4):
                            i = ti * 4 + c
                            mms.append((lambda b, c=c: banks[b][:, c * 128:(c + 1) * 128],
                                        vn_ap(i, hd * 128, 128), wspT[:, hd, :], True, True))
                        b = pe_job(mms, [r_vn[ti * 4 + c] for c in range(4)] + [r_wspT])
                        sp_ap, sp_res = ringD.next()
                        for c in range(4):
                            emit(DVE, lambda b=b, c=c, sp_ap=sp_ap: DVE.eng.scalar_tensor_tensor(
                                out=sp_ap[:, c * 128:(c + 1) * 128], in0=banks[b][:, c * 128:(c + 1) * 128],
                                scalar=vecs[:, V_LNG + hd:V_LNG + hd + 1], in1=biast[:, hd, :],
                                op0=ALU.mult, op1=ALU.add),
                                reads=[r_bank[b], r_const, r_biast], writes=[sp_res])
                        u_ap, u_res = ut[t]
                        emit(DVE, lambda sp_ap=sp_ap, u_ap=u_ap, t=t: DVE.eng.tensor_tensor(
                            out=gb_ap(hd, lc(t), 512), in0=sp_ap[:, :], in1=u_ap[:, :], op=ALU.mult),
                            reads=[sp_res, u_res], writes=[r_gb[hd][ti]])
            for m in range(8):
                uid = plan["p3"][m]
                s = load_unit(uid)
                for ti, t in enumerate(mt):
                    c0, n = TILES[t]
                    l0 = lc(t)
                    rh = [r_h[k][t] for k in range(DC)]
                    b = proj_job(s, uid, 0, 8, lambda k: h[:, k, c0:c0 + n], rh, n)
                    sa, sares = ringA.next()
                    emit(ACT, lambda b=b, sa=sa: ACT.eng.activation(out=sa[:, :], in_=banks[b][:, 0:512],
                                                                     func=AF.Sigmoid),
                         reads=[r_bank[b]], writes=[sares])
                    b = proj_job(s, uid, 1, 8, lambda k: h[:, k, c0:c0 + n], rh, n)
                    sb, sbres = ringA.next()
                    emit(ACT, lambda b=b, sb=sb: ACT.eng.activation(out=sb[:, :], in_=banks[b][:, 0:512],
                                                                     func=AF.Sigmoid),
                         reads=[r_bank[b]], writes=[sbres])
                    b = proj_job(s, uid, 2, 8, lambda k: za_ap(k, l0, 512), [r_za[k][ti] for k in range(8)], n)
                    t1, t1res = ringD.next()
                    emit(DVE, lambda b=b, sa=sa, t1=t1: DVE.eng.tensor_tensor(
                        out=t1[:, :], in0=banks[b][:, 0:512], in1=sa[:, :], op=ALU.mult),
                        reads=[r_bank[b], sares], writes=[t1res])
                    b = proj_job(s, uid, 3, 8, lambda k: gb_ap(k, l0, 512), [r_gb[k][ti] for k in range(8)], n)
                    t2, t2res = ringD.next()
                    emit(DVE, lambda b=b, sb=sb, t2=t2: DVE.eng.tensor_tensor(
                        out=t2[:, :], in0=banks[b][:, 0:512], in1=sb[:, :], op=ALU.mult),
                        reads=[r_bank[b], sbres], writes=[t2res])
                    emit(DVE, lambda t1=t1, t2=t2, l0=l0: DVE.eng.tensor_tensor(
                        out=mg_ap(m, l0, 512), in0=t1[:, :], in1=t2[:, :], op=ALU.add),
                        reads=[t1res, t2res], writes=[r_mg[m][ti]] + r_vn)
            se = [load_unit(plan["mo"][q]) for q in range(2)]
            for ti, t in enumerate(mt):
                c0, n = TILES[t]
                l0 = lc(t)
                for q in range(2):
                    uid = plan["mo"][q]
                    s = se[q]
                    for mi in range(4):
                        m = 4 * q + mi
                        b = proj_job(s, uid, mi, 8, lambda k: mg_ap(k, l0, 512), [r_mg[k][ti] for k in range(8)], n)
                        emit(DVE, lambda b=b, m=m: DVE.eng.scalar_tensor_tensor(
                            out=x[:, m, c0:c0 + n], in0=banks[b][:, 0:n], scalar=v_gtw(1, m),
                            in1=x[:, m, c0:c0 + n], op0=ALU.mult, op1=ALU.add),
                            reads=[r_bank[b], r_x[m][t], r_adaG[1]], writes=[r_x[m][t]])
                        sq_emit(m, t)
                nxt.tile_done(t)

        def phase_fence(res_lists):
            pass

        ALL5 = [0, 1, 2, 3, 4]
        MAIN = [0, 1, 2, 3]

        def alias_fence():
            allres = ([r for row in r_sq for r in row] + [r for row in r_g for r in row] + r_vn +
                      [r for row in r_gb for r in row] + [r for row in r_za for r in row] +
                      [r for row in r_mg for r in row])
            toks = []
            wtoks = []
            for r in allres:
                toks.extend(r.r)
                if r.w is not None:
                    wtoks.append(r.w)
            best = {}
            for (c, v) in toks + wtoks:
                best[c] = max(best.get(c, 0), v)
            comp = [(c, v) for c, v in best.items()]
            for r in allres:
                r.r = list(comp)

        def spatial_setup():
            emit(POOL, lambda: POOL.eng.affine_select(
                out=wsp_f.rearrange("p (a b) -> p a b", a=8), in_=wsp_f.rearrange("p (a b) -> p a b", a=8),
                pattern=[[0, 8], [1, 128]], compare_op=ALU.is_ge, fill=0.0, base=0, channel_multiplier=-1),
                reads=[r_wsp], writes=[r_wsp])
            emit(DVE, lambda: DVE.eng.tensor_copy(out=wspT[:].rearrange("p a b -> p (a b)"), in_=wsp_f),
                 reads=[r_wsp], writes=[r_wspT])
            for q in range(2):
                mms = []
                for i in range(4):
                    hd = 4 * q + i
                    mms.append((lambda b, i=i: banks[b][:, i * 128:(i + 1) * 128], ones_f[:, :],
                                wsp_f[:, hd * 128:(hd + 1) * 128], True, True))
                b = pe_job(mms, [r_ones, r_wsp])
                for i in range(4):
                    hd = 4 * q + i
                    emit(DVE, lambda b=b, i=i, hd=hd: DVE.eng.scalar_tensor_tensor(
                        out=biast[:, hd, :], in0=banks[b][:, i * 128:(i + 1) * 128],
                        scalar=vecs[:, V_LNB + hd:V_LNB + hd + 1], in1=bsp_b[:, hd * 128:(hd + 1) * 128],
                        op0=ALU.mult, op1=ALU.add),
                        reads=[r_bank[b], r_const, r_bspb], writes=[r_biast])
            r_regF_phase.r = list(r_wsp.r) + list(r_bspb.r)
            r_regF_phase.w = r_wsp.w

        n1 = NormPipe(0)
        load_x(0)
        for j in range(DC):
            sq_emit(j, 0)
        n1._ab(0, 6)
        ada_units(0, [0, 1])
        load_x(1, after=ada_tok[(0, 1)])
        for j in range(DC):
            sq_emit(j, 1)
        n1._ab(1, 7)
        spatial_setup()
        ada_units(0, [2])
        load_x(2, after=ada_tok[(0, 2)])
        ada_units(0, [3])
        load_x(3, after=ada_tok[(0, 3)])
        load_x(4)
        pro_rr = {}
        for t, (o, w) in zip((2, 3, 4), ((0, 512), (512, 512), (1024, 2))):
            for j in range(DC):
                sq_emit(j, t)
            dres = Res()
            dst = regF[:, o:o + 512] if w == 512 else regF[:, o:o + 2]
            n1._ab_sb(t, dst, dres, [r_regF_phase])
            pro_rr[t] = (dst, dres)
        n1._c(0, banks[6], r_bank[6])
        n1._c(1, banks[7], r_bank[7])

        def pro_pre_tile(t):
            if t + 2 <= 4:
                dst, dres = pro_rr[t + 2]
                n1._c(t + 2, dst, dres, extra_r=[r_regF_phase])

        ada_q = [(0, 4), (0, 5)] + [(1, q) for q in range(6)] + [(2, q) for q in range(6)]

        def pop_ada():
            if ada_q:
                p, q = ada_q.pop(0)
                ada_units(p, [q])

        n2 = NormPipe(1)
        ffn("ffn1", 0, ALL5, n2, pre_tile=pro_pre_tile, post_unit=pop_ada)
        assert not ada_q
        for s_ in range(2):
            r_zb[s_].r.extend(r_regF_phase.r)
            if r_regF_phase.w is not None:
                r_zb[s_].r.append(r_regF_phase.w)
        alias_fence()
        n3 = NormPipe(2)
        mix_half(0, n3)
        alias_fence()
        mix_half(1, n3)
        n3.flush()
        alias_fence()
        nf = NormPipe(None, final=True)
        ffn("ffn2", 2, MAIN, nf)
        streams["sp"].append({"m": "wait_ge", "a": (c_st.sem, c_st.val), "kw": {}, "inc": None})

        block = E(nc.Block())

        def replay(eng, items):
            for it in items:
                ins = getattr(eng, it["m"])(*it["a"], **it["kw"])
                if it["inc"] is not None:
                    ins.then_inc(*it["inc"])

        @block.tensor
        def _(e):
            replay(e, streams["pe"])

        @block.scalar
        def _(e):
            replay(e, streams["act"])

        @block.vector
        def _(e):
            replay(e, streams["dve"])

        @block.gpsimd
        def _(e):
            replay(e, streams["pool"])

        @block.sync
        def _(e):
            replay(e, streams["sp"])
    return nc


def kernel(x, c, w_ada, b_ada, g_ffn1, w_ffn1_in, w_ffn1_out, g_mix, w_mix_in, conv_w, ln_v_g, ln_v_b,
           w_spatial, b_spatial, w_a_out, w_b_out, w_mix_out, g_ffn2, w_ffn2_in, w_ffn2_out, g_final):
    f = lambda a: np.asarray(a, dtype=np.float32)
    x = f(x)
    W = {k: f(v) for k, v in dict(w_ada=w_ada, w_ffn1_in=w_ffn1_in, w_ffn1_out=w_ffn1_out, w_mix_in=w_mix_in,
                                  w_a_out=w_a_out, w_b_out=w_b_out, w_mix_out=w_mix_out, w_ffn2_in=w_ffn2_in,
                                  w_ffn2_out=w_ffn2_out).items()}
    stream, units, plan = build_weight_stream(W)
    fm = lambda v: f(v).reshape(8, 128).T
    wsp = np.ascontiguousarray(f(w_spatial).transpose(2, 0, 1))
    bsp = np.ascontiguousarray(np.broadcast_to(f(b_spatial).reshape(1, 1024), (128, 1024)))
    cw = f(conv_w)
    in_maps = []
    per_b = NCORES // x.shape[0]
    for core in range(NCORES):
        b = core // per_b
        s0 = (core % per_b) * TOK
        xs = x[b, s0:s0 + TOK, :]
        xt = np.zeros((128, DC, TH), np.float32)
        xt[:, :, :TOK] = xs.reshape(TOK, DC, 128).transpose(2, 1, 0)
        hmv = 0.0
        if s0 > 0:
            xt[:, :, TOK:TH] = x[b, s0 - 2:s0, :].reshape(2, DC, 128).transpose(2, 1, 0)
            hmv = 1.0
        vecs = np.zeros((128, NV), np.float32)
        vecs[:, V_G1:V_G1 + 8] = fm(g_ffn1)
        vecs[:, V_GM:V_GM + 8] = fm(g_mix)
        vecs[:, V_G2:V_G2 + 8] = fm(g_ffn2)
        vecs[:, V_GF:V_GF + 8] = fm(g_final)
        vecs[:, V_LNG:V_LNG + 8] = fm(ln_v_g)
        vecs[:, V_LNB:V_LNB + 8] = fm(ln_v_b)
        for k in range(3):
            vecs[:, V_CW + 8 * k:V_CW + 8 * k + 8] = fm(cw[k])
        vecs[:, V_BADA:V_BADA + 72] = f(b_ada).reshape(72, 128).T
        vecs[:, V_C:V_C + 8] = fm(f(c)[b])
        in_maps.append({"xT": xt, "vecs": vecs, "wsp": wsp, "bsp": bsp,
                        "hmask": np.full((128, 1), hmv, np.float32), "wst": stream})
    nc = build_program(units, plan, stream.shape[0])
    res = run_bass_kernel_spmd(nc, in_maps, core_ids=list(range(NCORES)))
    out = np.empty((x.shape[0], SEQ, D), np.float32)
    for core in range(NCORES):
        b = core // per_b
        s0 = (core % per_b) * TOK
        yt = np.asarray(res.results[core]["yT"])
        out[b, s0:s0 + TOK, :] = yt.transpose(2, 1, 0).reshape(TOK, D)
    return out
```
